# Optimizing a Trainium2 kernel written in Bass

```python
import functools
import jax
import jax.numpy as jnp
from jax import lax
import numpy as np


D_MODEL = 4096
BATCH = 4
SEQ = 2048
DEPTH = 1

HEAD_DIM = 128
N_ATTN_HEADS = 16
ATTN_WIDTH = N_ATTN_HEADS * HEAD_DIM
MOBA_BLOCK = 256
MOBA_TOP_K = 3
MOBA_Q_SUB = 16
ROPE_THETA = 10000.0
CONV_DIM = 2048
CONV_WIDTH = 31
D_FF = 11008
FFN_CONV_WIDTH = 3
EPS = 1e-6
IN_WIDTH = 3 * ATTN_WIDTH + 2 * CONV_DIM + 2 * D_MODEL

kernel_name = "moba_conformer_gated_hybrid"


def rms_norm(x, g):
    xf = x.astype(jnp.float32)
    y = xf * lax.rsqrt(jnp.mean(xf * xf, axis=-1, keepdims=True) + EPS)
    return (y * g.astype(jnp.float32)).astype(x.dtype)


def layer_norm(x, g, b):
    xf = x.astype(jnp.float32)
    mu = jnp.mean(xf, axis=-1, keepdims=True)
    xc = xf - mu
    var = jnp.mean(xc * xc, axis=-1, keepdims=True)
    y = xc * lax.rsqrt(var + EPS) * g.astype(jnp.float32) + b.astype(jnp.float32)
    return y.astype(x.dtype)


def rope(x, positions):
    half = HEAD_DIM // 2
    inv_freq = ROPE_THETA ** (-jnp.arange(half, dtype=jnp.float32) / half)
    ang = positions.astype(jnp.float32)[..., None] * inv_freq
    cos = jnp.cos(ang)[:, None]
    sin = jnp.sin(ang)[:, None]
    xf = x.astype(jnp.float32)
    x1, x2 = xf[..., :half], xf[..., half:]
    out = jnp.concatenate([x1 * cos - x2 * sin, x2 * cos + x1 * sin], axis=-1)
    return out.astype(x.dtype)


def causal_depthwise_conv(x, w, b):
    width = w.shape[0]
    y = lax.conv_general_dilated(
        x, w[:, None, :].astype(x.dtype), window_strides=(1,),
        padding=[(width - 1, 0)], dimension_numbers=('NWC', 'WIO', 'NWC'),
        feature_group_count=x.shape[-1])
    return y + b.astype(x.dtype)


def gather_blocks(blocks, idx):
    return jax.vmap(jax.vmap(lambda kb, ix: kb[ix]))(blocks, idx)


def moba_attention(q, k, v):
    B, H, S, hd = q.shape
    L = MOBA_BLOCK
    nb = -(-S // L)
    pad = nb * L - S
    if pad:
        cfg = ((0, 0), (0, 0), (0, pad), (0, 0))
        q, k, v = jnp.pad(q, cfg), jnp.pad(k, cfg), jnp.pad(v, cfg)
    q = q * jnp.asarray(HEAD_DIM ** -0.5, q.dtype)
    kb = k.reshape(B, H, nb, L, hd)
    vb = v.reshape(B, H, nb, L, hd)
    k_mean = jnp.mean(kb.astype(jnp.float32), axis=3)
    causal_local = jnp.tril(jnp.ones((L, L), dtype=bool))
    n_sub = L // MOBA_Q_SUB
    neg = jnp.float32(-jnp.inf)
    outs = []
    for i in range(nb):
        qi = q[:, :, i * L:(i + 1) * L]
        ki, vi = kb[:, :, i], vb[:, :, i]
        n_sel = min(MOBA_TOP_K, i)
        if n_sel == 0:
            s = jnp.einsum('bhqd,bhkd->bhqk', qi, ki).astype(jnp.float32)
            p = jax.nn.softmax(jnp.where(causal_local, s, neg), axis=-1).astype(v.dtype)
            outs.append(jnp.einsum('bhqk,bhkd->bhqd', p, vi))
            continue
        gate = jnp.einsum('bhqd,bhnd->bhqn', qi.astype(jnp.float32), k_mean[:, :, :i])
        _, idx = lax.top_k(gate, n_sel)
        kp, vp = kb[:, :, :i], vb[:, :, :i]
        q_sub = jnp.moveaxis(qi.reshape(B, H, n_sub, MOBA_Q_SUB, hd), 2, 0)
        idx_sub = jnp.moveaxis(idx.reshape(B, H, n_sub, MOBA_Q_SUB, n_sel), 2, 0)
        mask_sub = causal_local.reshape(n_sub, MOBA_Q_SUB, L)

        def step(args, kp=kp, vp=vp, ki=ki, vi=vi, n_sel=n_sel):
            qc, ic, mc = args
            kg = gather_blocks(kp, ic)
            vg = gather_blocks(vp, ic)
            s_sel = jnp.einsum('bhqd,bhqnld->bhqnl', qc, kg).astype(jnp.float32)
            s_sel = s_sel.reshape(B, H, MOBA_Q_SUB, n_sel * L)
            s_loc = jnp.einsum('bhqd,bhld->bhql', qc, ki).astype(jnp.float32)
            s_loc = jnp.where(mc, s_loc, neg)
            p = jax.nn.softmax(jnp.concatenate([s_sel, s_loc], axis=-1), axis=-1).astype(vi.dtype)
            p_sel = p[..., :n_sel * L].reshape(B, H, MOBA_Q_SUB, n_sel, L)
            p_loc = p[..., n_sel * L:]
            return (jnp.einsum('bhqnl,bhqnld->bhqd', p_sel, vg)
                    + jnp.einsum('bhql,bhld->bhqd', p_loc, vi))

        o = lax.map(step, (q_sub, idx_sub, mask_sub))
        outs.append(jnp.moveaxis(o, 0, 2).reshape(B, H, L, hd))
    return jnp.concatenate(outs, axis=2)[:, :, :S]


def hybrid_mixer(xn, positions, w_in, q_norm_g, k_norm_g, w_o_attn, conv_w, conv_b,
                 conv_ln_g, conv_ln_b, w_o_conv, w_out):
    B, S, _ = xn.shape
    proj = xn @ w_in
    q, k, v, u, gates = jnp.split(
        proj, [ATTN_WIDTH, 2 * ATTN_WIDTH, 3 * ATTN_WIDTH, 3 * ATTN_WIDTH + 2 * CONV_DIM], axis=-1)

    def heads(t):
        return t.reshape(B, S, N_ATTN_HEADS, HEAD_DIM).transpose(0, 2, 1, 3)

    q = rope(rms_norm(heads(q), q_norm_g), positions)
    k = rope(rms_norm(heads(k), k_norm_g), positions)
    attn = moba_attention(q, k, heads(v))
    y_attn = attn.transpose(0, 2, 1, 3).reshape(B, S, ATTN_WIDTH) @ w_o_attn

    a, b = jnp.split(u, 2, axis=-1)
    h = a * jax.nn.sigmoid(b)
    h = causal_depthwise_conv(h, conv_w, conv_b)
    h = jax.nn.silu(layer_norm(h, conv_ln_g, conv_ln_b))
    y_conv = h @ w_o_conv

    g = jax.nn.sigmoid(gates.astype(jnp.float32)).astype(xn.dtype)
    g_attn, g_conv = jnp.split(g, 2, axis=-1)
    return (g_attn * y_attn + g_conv * y_conv) @ w_out


def conv_ffn(xn, w_ffn_in, ffn_conv_w, ffn_conv_b, w_ffn_out):
    h = causal_depthwise_conv(xn @ w_ffn_in, ffn_conv_w, ffn_conv_b)
    gate, up = jnp.split(h, 2, axis=-1)
    return (jax.nn.silu(gate) * up) @ w_ffn_out


def setup_inputs(seed: int = 0) -> dict:
    key = jax.random.key(seed)
    ks = jax.random.split(key, 20)
    f32 = jnp.float32

    def w(k, shape, fan_in):
        return jax.random.normal(k, shape, f32) * (fan_in ** -0.5)

    def gain(k, shape):
        return 1.0 + 0.01 * jax.random.normal(k, shape, f32)

    def bias(k, shape):
        return 0.01 * jax.random.normal(k, shape, f32)

    x = jax.random.normal(ks[0], (BATCH, SEQ, D_MODEL), f32)
    positions = jnp.broadcast_to(jnp.arange(SEQ, dtype=jnp.int32), (BATCH, SEQ))
    return {
        'x': x,
        'positions': positions,
        'norm1_g': gain(ks[1], (DEPTH, D_MODEL)),
        'w_in': w(ks[2], (DEPTH, D_MODEL, IN_WIDTH), D_MODEL),
        'q_norm_g': gain(ks[3], (DEPTH, HEAD_DIM)),
        'k_norm_g': gain(ks[4], (DEPTH, HEAD_DIM)),
        'w_o_attn': w(ks[5], (DEPTH, ATTN_WIDTH, D_MODEL), ATTN_WIDTH),
        'conv_w': w(ks[6], (DEPTH, CONV_WIDTH, CONV_DIM), CONV_WIDTH),
        'conv_b': bias(ks[7], (DEPTH, CONV_DIM)),
        'conv_ln_g': gain(ks[8], (DEPTH, CONV_DIM)),
        'conv_ln_b': bias(ks[9], (DEPTH, CONV_DIM)),
        'w_o_conv': w(ks[10], (DEPTH, CONV_DIM, D_MODEL), CONV_DIM),
        'w_out': w(ks[11], (DEPTH, D_MODEL, D_MODEL), D_MODEL),
        'norm2_g': gain(ks[12], (DEPTH, D_MODEL)),
        'w_ffn_in': w(ks[13], (DEPTH, D_MODEL, 2 * D_FF), D_MODEL),
        'ffn_conv_w': w(ks[14], (DEPTH, FFN_CONV_WIDTH, 2 * D_FF), FFN_CONV_WIDTH),
        'ffn_conv_b': bias(ks[15], (DEPTH, 2 * D_FF)),
        'w_ffn_out': w(ks[16], (DEPTH, D_FF, D_MODEL), D_FF),
    }


def reference(x, positions, norm1_g, w_in, q_norm_g, k_norm_g, w_o_attn, conv_w, conv_b,
              conv_ln_g, conv_ln_b, w_o_conv, w_out, norm2_g, w_ffn_in, ffn_conv_w,
              ffn_conv_b, w_ffn_out):
    for l in range(DEPTH):
        xn = rms_norm(x, norm1_g[l])
        x = x + hybrid_mixer(xn, positions, w_in[l], q_norm_g[l], k_norm_g[l], w_o_attn[l],
                             conv_w[l], conv_b[l], conv_ln_g[l], conv_ln_b[l], w_o_conv[l],
                             w_out[l])
        xn = rms_norm(x, norm2_g[l])
        x = x + conv_ffn(xn, w_ffn_in[l], ffn_conv_w[l], ffn_conv_b[l], w_ffn_out[l])
    return x
```

```python
import os
import numpy as np
import concourse.bass as bass
import concourse.mybir as mybir
from concourse.bass_utils import run_bass_kernel_spmd

F32 = mybir.dt.float32
BF16 = mybir.dt.bfloat16
I32 = mybir.dt.int32
AF = mybir.ActivationFunctionType
ALU = mybir.AluOpType
AX = mybir.AxisListType

D = 4096
NH = 16
HD = 128
AW = 2048
CD = 2048
CW = 31
DFF = 11008
INW = 18432
EPS = 1e-6
E = 32
NO = 1024
NP = 1024
NT = NO + E
NF = NO + 2
BIG = 30000.0
TT3 = [(0, 352), (352, 352), (704, 352)]
TF3 = [(0, 342), (342, 342), (684, 342)]
T2 = [(0, 512), (512, 512)]
KC = 32
NSLOT = 12
KPIECE = 4
TWO_PI = 6.283185307179586
C1 = 6.28125
C2 = TWO_PI - C1

_cols = {}
_off = 0
for _n, _w in [("g1", 32), ("g2", 32), ("qg", 1), ("kg", 1), ("convw", 16 * 31), ("convb", 16),
               ("lng", 16), ("lnb", 16), ("fcw", 172 * 3), ("fcb", 172), ("invf", 1), ("flag", 1),
               ("mask", 32)]:
    _cols[_n] = _off
    _off += _w
NPRM = _off

DBG = os.environ.get("MK_DBG", "")


class Res:
    __slots__ = ("w", "ws", "r")

    def __init__(self):
        self.w = None
        self.ws = {}
        self.r = {}


class Chan:
    def __init__(self, sem):
        self.sem = sem
        self.n = 0


class Sched:
    ENGS = ("pe", "act", "dve", "pool", "sp")

    def __init__(self, sems):
        self.sem = sems
        self.cnt = {e: 0 for e in self.ENGS}
        self.ops = {e: [] for e in self.ENGS}
        self.seen = {e: {} for e in self.ENGS}
        self.lasttok = {e: None for e in self.ENGS}
        self.lastdma = []

    def _waits(self, eng, need):
        out = []
        best = {}
        for tok in need:
            if tok is None:
                continue
            key, sem, val, teng = tok
            if teng == "pe" and eng == "pe":
                continue
            if best.get(key, (None, 0))[1] < val:
                best[key] = (sem, val)
        seen = self.seen[eng]
        for key, (sem, val) in best.items():
            if seen.get(key, 0) < val:
                seen[key] = val
                out.append((sem, val))
        return out

    def _deps(self, reads, writes):
        need = []
        for r in reads:
            need.extend(r.ws.values())
        for w in writes:
            need.extend(w.ws.values())
            for key, (sem, val, teng) in w.r.items():
                need.append((key, sem, val, teng))
        return need

    def _commit(self, tok, reads, writes):
        key, sem, val, teng = tok
        for r in reads:
            cur = r.r.get(key)
            if cur is None or cur[1] < val:
                r.r[key] = (sem, val, teng)
        for w in writes:
            w.w = tok
            if teng == "dma":
                w.ws = {k: v for k, v in w.ws.items() if v[3] == "dma" and k != key}
            else:
                w.ws = {}
            w.ws[key] = tok
            w.r = {}

    def op(self, eng, fn, reads=(), writes=()):
        waits = self._waits(eng, self._deps(reads, writes))
        self.cnt[eng] += 1
        tok = ("e_" + eng, self.sem[eng], self.cnt[eng], eng)
        self.ops[eng].append((waits, fn, (self.sem[eng], 1)))
        self._commit(tok, reads, writes)
        self.lasttok[eng] = tok
        return tok

    def dma(self, eng, chan, out, in_, reads=(), writes=(), **kw):
        waits = self._waits(eng, self._deps(reads, writes))
        chan.n += 16
        tok = ("c_%d" % id(chan), chan.sem, chan.n, "dma")
        self.ops[eng].append((waits, lambda e, o=out, i=in_, k=kw: e.dma_start(out=o, in_=i, **k), (chan.sem, 16)))
        self._commit(tok, reads, writes)
        self.lastdma.append(tok)
        return tok

    def wait_all(self, eng, toks):
        waits = self._waits(eng, toks)
        self.ops[eng].append((waits, None, None))

    def emit(self, eng, e):
        for waits, fn, inc in self.ops[eng]:
            for sem, val in waits:
                e.wait_ge(sem, val)
            if fn is not None:
                inst = fn(e)
                inst.then_inc(inc[0], inc[1])


def build_nc():
    nc = bass.Bass("TRN2", target_bir_lowering=False)
    dt = nc.dram_tensor
    xT_ext = dt("xT_ext", [D, NT], F32, kind="ExternalInput").ap()
    xT_prev = dt("xT_prev", [D, NP], F32, kind="ExternalInput").ap()
    pos_ext = dt("pos_ext", [128, NT], I32, kind="ExternalInput").ap()
    pos_prev = dt("pos_prev", [128, NP], I32, kind="ExternalInput").ap()
    prm_d = dt("prm", [128, NPRM], F32, kind="ExternalInput").ap()
    cst_d = dt("cst", [128, 3 * 128], F32, kind="ExternalInput").ap()
    row_d = dt("rowqk", [1, 256], F32, kind="ExternalInput").ap()
    w_in = dt("w_in", [D, INW], F32, kind="ExternalInput").ap()
    w_oa = dt("w_o_attn", [AW, D], F32, kind="ExternalInput").ap()
    w_oc = dt("w_o_conv", [CD, D], F32, kind="ExternalInput").ap()
    w_out = dt("w_out", [D, D], F32, kind="ExternalInput").ap()
    w_fi = dt("w_ffn_in", [D, 2 * DFF], F32, kind="ExternalInput").ap()
    w_fo = dt("w_ffn_out", [DFF, D], F32, kind="ExternalInput").ap()
    outT = dt("outT", [D, NO], F32, kind="ExternalOutput").ap()
    k_dram = dt("k_scr", [NH, 128, NP], BF16, kind="Internal").ap()
    v_dram = dt("v_scr", [8, 128, 8, 256], BF16, kind="Internal").ap()
    c_dram = dt("c_scr", [16, 128, NF], F32, kind="Internal").ap()
    g_dram = dt("g_scr", [64, 128, NT], F32, kind="Internal").ap()
    dbg = {}
    if DBG:
        dbg["attn"] = dt("dbg_attn", [128, 16, NT], F32, kind="ExternalOutput").ap()
        dbg["hc"] = dt("dbg_hc", [128, 16, NT], F32, kind="ExternalOutput").ap()
        dbg["z"] = dt("dbg_z", [128, 32, NT], F32, kind="ExternalOutput").ap()
        dbg["xn"] = dt("dbg_xn", [128, 32, NT], F32, kind="ExternalOutput").ap()

    def wv(W):
        return W.rearrange("(k p) c -> p k c", p=128)

    from contextlib import ExitStack
    with ExitStack() as es:
        def sb(name, shape, dtype):
            return es.enter_context(nc.sbuf_tensor(name, shape, dtype))

        A = sb("A", [128, 32, NT], BF16)
        BC = sb("BC", [128, 32, NT], BF16)
        ring = sb("ring", [128, NSLOT, KPIECE, 256], BF16)
        NFS = 4
        fring = sb("fring", [128, NFS, KPIECE, 128], BF16)
        prm = sb("prm_s", [128, NPRM], F32)
        cst = sb("cst_s", [128, 3 * 128], F32)
        cbf = sb("cbf_s", [128, 3 * 128], BF16)
        onesf = sb("onesf", [128, 128], F32)
        row = sb("row_s", [1, 264], F32)
        small = sb("small_s", [128, 64], F32)
        kmean = sb("kmean", [128, NH, 8], F32)
        xst = sb("xst", [128, 2, NT], F32)
        f0 = sb("f0", [128, NT], F32)
        f1 = sb("f1", [128, NT], F32)
        f2 = sb("f2", [128, NT], F32)
        f3 = sb("f3", [128, NT], F32)
        f4 = sb("f4", [128, NT], F32)
        f5 = sb("f5", [128, NT], F32)
        qb = sb("qb", [128, NT], BF16)
        flat = BC[:, 16:32, :].rearrange("p a b -> p (a b)")
        cosT = flat[:, 0:2 * NT].bitcast(F32)
        sinT = flat[:, 2 * NT:4 * NT].bitcast(F32)
        o_ = 4 * NT
        Kc = flat[:, o_:o_ + 4096].rearrange("p (h t) -> p h t", h=2)
        Vc = flat[:, o_ + 4096:o_ + 8192].rearrange("p (t c) -> p t c", c=256)
        pt = flat[:, o_ + 8192:o_ + 9216].rearrange("p (a b) -> p a b", a=2)
        dj = flat[:, o_ + 9216:o_ + 10240].rearrange("p (a b) -> p a b", a=8)
        dj2 = flat[:, o_ + 10240:o_ + 11264].rearrange("p (a b) -> p a b", a=8)
        djs = [dj, dj2]
        gsm = sb("gsm", [128, 64], F32)
        rs = sb("rs", [128, 2, 128], F32)
        ps = es.enter_context(nc.psum_tensor("ps", [128, 8, 512], F32))
        sem_names = ["pe", "act", "dve", "pool", "sp"]
        sems = {n: es.enter_context(nc.semaphore("s_" + n)) for n in sem_names}
        chan_sems = [es.enter_context(nc.semaphore("c_%d" % i)) for i in range(70)]
        chan_i = [0]

        def newchan():
            c = Chan(chan_sems[chan_i[0]])
            chan_i[0] += 1
            return c

        S = Sched(sems)

        rA = [Res() for _ in range(32)]
        rBC = [Res() for _ in range(32)]
        rslot = [Res() for _ in range(NSLOT)]
        cslot = [newchan() for _ in range(NSLOT)]
        rxst = [Res(), Res()]
        rfs = [Res() for _ in range(4)]
        cfs = [newchan() for _ in range(4)]
        cxst = [newchan(), newchan()]
        rbank = [Res() for _ in range(8)]
        bank_open = [False] * 8
        bank_ptr = [0]
        rf = {n: Res() for n in ["f0", "f1", "f2", "f3", "f4", "f5", "qb", "cos", "sin", "prm", "cst", "cbf",
                                 "onesf", "row", "small", "kmean", "pt0", "pt1", "dj", "dj2", "gsm", "gsm2", "rs0", "rs1",
                                 "Kc0", "Kc1", "Vc_prev", "Vc_own", "kdram", "vdram", "cdram", "gdram", "out"]}
        rKc = [rf["Kc0"], rf["Kc1"]]
        rpt = [rf["pt0"], rf["pt1"]]
        rrs = [rf["rs0"], rf["rs1"]]
        cmisc = newchan()
        ckv = [newchan(), newchan(), newchan()]
        ckvk = [newchan(), newchan()]
        cout = [newchan(), newchan()]
        cspill = [newchan(), newchan()]
        cdbg = newchan()
        cnorm = [newchan(), newchan()]
        cgl = [newchan(), newchan()]
        all_out_toks = []

        bank_stamp = [0] * 8
        stamp_ctr = [0]

        def bank_alloc():
            best = None
            for b in range(8):
                if not bank_open[b] and (best is None or bank_stamp[b] < bank_stamp[best]):
                    best = b
            if best is None:
                raise RuntimeError("no psum bank")
            bank_open[best] = True
            return best

        def bank_free(b):
            bank_open[b] = False
            stamp_ctr[0] += 1
            bank_stamp[b] = stamp_ctr[0]

        slot_ptr = [0]

        def wload(Wv, k0, kc, c0):
            s = slot_ptr[0]
            slot_ptr[0] = (s + 1) % NSLOT
            S.dma("pool", cslot[s], ring[:, s, 0:kc, :], Wv[:, k0:k0 + kc, c0:c0 + 256], writes=[rslot[s]])
            return s

        def wpieces(Wv, kchunks, c0):
            out = []
            k0 = 0
            assert (kchunks + KPIECE - 1) // KPIECE <= NSLOT
            while k0 < kchunks:
                kc = min(KPIECE, kchunks - k0)
                out.append((wload(Wv, k0, kc, c0), k0, kc))
                k0 += kc
            return out

        def mm_group(out_ap, pairs, reads, bank):
            def fn(pe, out_ap=out_ap, pairs=pairs):
                n = len(pairs)
                inst = None
                for i, (l, r) in enumerate(pairs):
                    inst = pe.matmul(out_ap, l, r, start=(i == 0), stop=(i == n - 1))
                return inst
            return S.op("pe", fn, reads=reads, writes=[rbank[bank]])

        def proj_A(pieces, f, X, xres, kbase, tts):
            bl = [bank_alloc() for _ in tts]
            nk = sum(kc for (_, _, kc) in pieces)
            kdone = 0
            for pi_, (s, k0, kc) in enumerate(pieces):
                def fn(pe, s=s, k0=k0, kc=kc, kdone=kdone, f=f, X=X, kbase=kbase, tts=tts, bl=bl, nk=nk):
                    inst = None
                    for kk in range(kc):
                        ki = kdone + kk
                        for ti, (t0, tn) in enumerate(tts):
                            inst = pe.matmul(ps[:, bl[ti], 0:tn], ring[:, s, kk, f * 128:(f + 1) * 128],
                                             X[:, kbase + k0 + kk, t0:t0 + tn], start=(ki == 0), stop=(ki == nk - 1))
                    return inst
                S.op("pe", fn, reads=[rslot[s]] + xres[k0:k0 + kc], writes=[rbank[b] for b in bl])
                kdone += kc
            return bl

        GATE0 = 3 * AW + 2 * CD
        NKP = KC // KPIECE

        def gate_filler_gen():
            units = [(gch, p) for gch in range(64) for p in range(NKP)]
            loaded = {}

            def load(ui):
                if ui < len(units) and ui not in loaded:
                    gch, p = units[ui]
                    sl = ui % NFS
                    S.dma("pool", cfs[sl], fring[:, sl, :, :],
                          wv(w_in)[:, p * KPIECE:(p + 1) * KPIECE, GATE0 + 128 * gch:GATE0 + 128 * (gch + 1)], writes=[rfs[sl]])
                    loaded[ui] = sl
            bl = None
            for ui, (gch, p) in enumerate(units):
                for a_ in range(NFS - 1):
                    load(ui + a_)
                if p == 0:
                    bl = [bank_alloc() for _ in TT3]
                sl = loaded[ui]

                def fn(pe, sl=sl, p=p, bl=bl):
                    inst = None
                    for kk in range(KPIECE):
                        ki = p * KPIECE + kk
                        for ti, (t0, tn) in enumerate(TT3):
                            inst = pe.matmul(ps[:, bl[ti], 0:tn], fring[:, sl, kk, :], A[:, ki, t0:t0 + tn],
                                             start=(ki == 0), stop=(ki == KC - 1))
                    return inst
                S.op("pe", fn, reads=[rfs[sl]] + rA[p * KPIECE:(p + 1) * KPIECE], writes=[rbank[b] for b in bl])
                if p == NKP - 1:
                    gbuf, gres, gchan = gate_stage[gch % 2]
                    for bi, (t0, tn) in enumerate(TT3):
                        act(lambda e, bi=bi, t0=t0, tn=tn, bl=bl, gbuf=gbuf: e.activation(gbuf[:, t0:t0 + tn], ps[:, bl[bi], 0:tn], AF.Sigmoid),
                            reads=[rbank[bl[bi]]], writes=[gres])
                        bank_free(bl[bi])
                    S.dma("sp", gchan, g_dram[gch, :, :], gbuf[:, :], reads=[gres], writes=[rf["gdram"]])
                yield

        gate_stage = [(xst[:, 0, :], rxst[0], cspill[0]), (xst[:, 1, :], rxst[1], cspill[1])]
        filler_state = {"gen": None}

        def fill(n):
            g = filler_state["gen"]
            if g is None:
                return
            for _ in range(n):
                try:
                    next(g)
                except StopIteration:
                    filler_state["gen"] = None
                    return

        def dve(fn, reads=(), writes=()):
            return S.op("dve", fn, reads, writes)

        def act(fn, reads=(), writes=()):
            return S.op("act", fn, reads, writes)

        def pe(fn, reads=(), writes=()):
            return S.op("pe", fn, reads, writes)

        P = lambda n, i=0, w=1: prm[:, _cols[n] + i:_cols[n] + i + w]

        S.dma("sp", newchan(), prm[:, :], prm_d[:, :], writes=[rf["prm"]])
        S.dma("sp", newchan(), cst[:, :], cst_d[:, :], writes=[rf["cst"]])
        S.dma("sp", newchan(), row[:, 0:256], row_d[:, :], writes=[rf["row"]])
        ident_f = cst[:, 0:128]
        rot_f = cst[:, 128:256]
        tri_f = cst[:, 256:384]
        dve(lambda e: e.memset(onesf[:, :], 1.0), writes=[rf["onesf"]])
        dve(lambda e: e.tensor_copy(cbf[:, 0:128], cst[:, 0:128]), reads=[rf["cst"]], writes=[rf["cbf"]])
        dve(lambda e: e.memset(cbf[:, 128:256], 1.0), writes=[rf["cbf"]])
        dve(lambda e: e.tensor_copy(cbf[:, 256:384], cst[:, 256:384]), reads=[rf["cst"]], writes=[rf["cbf"]])
        ident_b = cbf[:, 0:128]
        ones_b = cbf[:, 128:256]
        tri_b = cbf[:, 256:384]
        dve(lambda e: e.tensor_reduce(row[:, 256:257], row[:, 0:128], AX.X, ALU.max, apply_absolute_value=True),
            reads=[rf["row"]], writes=[rf["row"]])
        dve(lambda e: e.tensor_reduce(row[:, 257:258], row[:, 128:256], AX.X, ALU.max, apply_absolute_value=True),
            reads=[rf["row"]], writes=[rf["row"]])
        dve(lambda e: e.tensor_tensor(row[:, 258:259], row[:, 256:257], row[:, 257:258], ALU.mult),
            reads=[rf["row"]], writes=[rf["row"]])
        b0 = bank_alloc()
        pe(lambda e: e.matmul(ps[:, b0, 0:2], onesf[0:1, :], row[0:1, 258:260], start=True, stop=True),
           reads=[rf["row"], rf["onesf"]], writes=[rbank[b0]])
        negc = small[:, 0:1]
        qgs = small[:, 1:2]
        dve(lambda e: e.tensor_scalar(negc, ps[:, b0, 0:1], -(128.0 ** 0.5), None, ALU.mult),
            reads=[rbank[b0]], writes=[rf["small"]])
        bank_free(b0)
        dve(lambda e: e.tensor_scalar(qgs, P("qg"), 128.0 ** -0.5, None, ALU.mult), reads=[rf["prm"]], writes=[rf["small"]])

        def rope_tables(pos_d, n):
            S.dma("sp", cxst[0], xst[:, 0, 0:n].bitcast(I32), pos_d[:, :], writes=[rxst[0]])
            dve(lambda e: e.tensor_copy(f0[:, 0:n], xst[:, 0, 0:n].bitcast(I32)), reads=[rxst[0]], writes=[rf["f0"]])
            dve(lambda e: e.tensor_scalar(f1[:, 0:n], f0[:, 0:n], P("invf"), None, ALU.mult),
                reads=[rf["f0"], rf["prm"]], writes=[rf["f1"]])
            dve(lambda e: e.tensor_scalar(f2[:, 0:n], f1[:, 0:n], 1.0 / TWO_PI, None, ALU.mult),
                reads=[rf["f1"]], writes=[rf["f2"]])
            dve(lambda e: e.tensor_copy(xst[:, 1, 0:n].bitcast(I32), f2[:, 0:n]), reads=[rf["f2"]], writes=[rxst[1]])
            dve(lambda e: e.tensor_copy(f2[:, 0:n], xst[:, 1, 0:n].bitcast(I32)), reads=[rxst[1]], writes=[rf["f2"]])
            dve(lambda e: e.scalar_tensor_tensor(f1[:, 0:n], f2[:, 0:n], -C1, f1[:, 0:n], ALU.mult, ALU.add),
                reads=[rf["f2"], rf["f1"]], writes=[rf["f1"]])
            dve(lambda e: e.scalar_tensor_tensor(f1[:, 0:n], f2[:, 0:n], -C2, f1[:, 0:n], ALU.mult, ALU.add),
                reads=[rf["f2"], rf["f1"]], writes=[rf["f1"]])
            PI = float(np.pi)

            def wrap(dst, dres, shift):
                dve(lambda e: e.tensor_scalar(dst[:, 0:n], f1[:, 0:n], shift, None, ALU.add), reads=[rf["f1"]], writes=[dres])
                dve(lambda e: e.tensor_scalar(f3[:, 0:n], dst[:, 0:n], PI, -TWO_PI, ALU.is_gt, ALU.mult), reads=[dres], writes=[rf["f3"]])
                dve(lambda e: e.tensor_tensor(dst[:, 0:n], dst[:, 0:n], f3[:, 0:n], ALU.add), reads=[dres, rf["f3"]], writes=[dres])
                dve(lambda e: e.tensor_scalar(f3[:, 0:n], dst[:, 0:n], -PI, TWO_PI, ALU.is_lt, ALU.mult), reads=[dres], writes=[rf["f3"]])
                dve(lambda e: e.tensor_tensor(dst[:, 0:n], dst[:, 0:n], f3[:, 0:n], ALU.add), reads=[dres, rf["f3"]], writes=[dres])
            wrap(f0, rf["f0"], 0.0)
            wrap(f2, rf["f2"], PI / 2)
            act(lambda e: e.activation(sinT[:, 0:n], f0[:, 0:n], AF.Sin), reads=[rf["f0"]], writes=[rf["sin"]])
            act(lambda e: e.activation(cosT[:, 0:n], f2[:, 0:n], AF.Sin), reads=[rf["f2"]], writes=[rf["cos"]])

        def colsum_to(dst, dres, src, sres, tts, scale, func_chain):
            for (t0, tn) in tts:
                b = bank_alloc()
                pe(lambda e, b=b, t0=t0, tn=tn: e.matmul(ps[:, b, 0:tn], onesf[:, :], src[:, t0:t0 + tn], start=True, stop=True),
                   reads=[sres, rf["onesf"]], writes=[rbank[b]])
                func_chain(b, t0, tn)
                bank_free(b)

        def rstd_chain(dst, dres, scale, eps):
            def chain(b, t0, tn):
                act(lambda e: e.activation(dst[:, t0:t0 + tn], ps[:, b, 0:tn], AF.Ln, bias=eps_ap, scale=scale),
                    reads=[rbank[b], rf["small"]], writes=[dres])
                act(lambda e: e.activation(dst[:, t0:t0 + tn], dst[:, t0:t0 + tn], AF.Exp, scale=-0.5),
                    reads=[dres], writes=[dres])
            return chain

        eps_ap = small[:, 2:3]
        dve(lambda e: e.memset(eps_ap, EPS), writes=[rf["small"]])
        dve(lambda e: e.memset(gsm[:, 8:11], -BIG / 2), writes=[rf["gsm"]])
        dve(lambda e: e.memset(gsm[:, 40:43], -BIG / 2), writes=[rf["gsm2"]])

        def norm_into_A(xT_d, n, tts, gname, dbg_name=None):
            stg = [(xst[:, 0, :], rxst[0], cxst[0]), (xst[:, 1, :], rxst[1], cxst[1]),
                   (f0, rf["f0"], cnorm[0]), (f1, rf["f1"], cnorm[1])]
            sqt = [(f5, rf["f5"]), (f2, rf["f2"]), (f3, rf["f3"])]
            nbk = [bank_alloc() for _ in tts]
            for k in range(KC):
                buf, res, ch = stg[k % 4]
                S.dma("sp", ch, buf[:, 0:n], xT_d[k * 128:(k + 1) * 128, :], writes=[res])
                tq, rq = sqt[k % 3]
                act(lambda e, buf=buf, tq=tq: e.activation(tq[:, 0:n], buf[:, 0:n], AF.Square), reads=[res], writes=[rq])

                def fn(e, tq=tq, k=k):
                    inst = None
                    for ti, (t0, tn) in enumerate(tts):
                        inst = e.matmul(ps[:, nbk[ti], 0:tn], onesf[:, :], tq[:, t0:t0 + tn], start=(k == 0), stop=(k == KC - 1))
                    return inst
                pe(fn, reads=[rq, rf["onesf"]], writes=[rbank[b] for b in nbk])
            chain = rstd_chain(f4, rf["f4"], 1.0 / D, EPS)
            for ti, (t0, tn) in enumerate(tts):
                chain(nbk[ti], t0, tn)
                bank_free(nbk[ti])
            for k in range(KC):
                buf, res, ch = stg[k % 4]
                S.dma("sp", ch, buf[:, 0:n], xT_d[k * 128:(k + 1) * 128, :], writes=[res])
                dve(lambda e, buf=buf, k=k: e.scalar_tensor_tensor(A[:, k, 0:n], buf[:, 0:n], P(gname, k), f4[:, 0:n], ALU.mult, ALU.mult),
                    reads=[res, rf["f4"], rf["prm"]], writes=[rA[k]])

        def qk_epilogue(banks, tts, gcol, n, out_f, out_f_res, out_b, out_b_res, out_b_off=0, b_src_off=0, b_n=None):
            for bi, (t0, tn) in enumerate(tts):
                act(lambda e, bi=bi, t0=t0, tn=tn: e.activation(f5[:, t0:t0 + tn], ps[:, banks[bi], 0:tn], AF.Square),
                    reads=[rbank[banks[bi]]], writes=[rf["f5"]])
            fill(2)
            for (t0, tn) in tts:
                b = bank_alloc()
                pe(lambda e, b=b, t0=t0, tn=tn: e.matmul(ps[:, b, 0:tn], onesf[:, :], f5[:, t0:t0 + tn], start=True, stop=True),
                   reads=[rf["f5"], rf["onesf"]], writes=[rbank[b]])
                act(lambda e, b=b, t0=t0, tn=tn: e.activation(f4[:, t0:t0 + tn], ps[:, b, 0:tn], AF.Ln, bias=eps_ap, scale=1.0 / HD),
                    reads=[rbank[b], rf["small"]], writes=[rf["f4"]])
                bank_free(b)
            act(lambda e: e.activation(f4[:, 0:n], f4[:, 0:n], AF.Exp, scale=-0.5), reads=[rf["f4"]], writes=[rf["f4"]])
            for bi, (t0, tn) in enumerate(tts):
                dve(lambda e, bi=bi, t0=t0, tn=tn: e.scalar_tensor_tensor(f3[:, t0:t0 + tn], ps[:, banks[bi], 0:tn], gcol,
                                                                          f4[:, t0:t0 + tn], ALU.mult, ALU.mult),
                    reads=[rbank[banks[bi]], rf["f4"], rf["small"], rf["prm"]], writes=[rf["f3"]])
                bank_free(banks[bi])
            fill(2)
            for (t0, tn) in tts:
                b = bank_alloc()
                pe(lambda e, b=b, t0=t0, tn=tn: e.matmul(ps[:, b, 0:tn], rot_f, f3[:, t0:t0 + tn], start=True, stop=True),
                   reads=[rf["f3"], rf["cst"]], writes=[rbank[b]])
                dve(lambda e, b=b, t0=t0, tn=tn: e.tensor_tensor(f5[:, t0:t0 + tn], ps[:, b, 0:tn], sinT[:, t0:t0 + tn], ALU.mult),
                    reads=[rbank[b], rf["sin"]], writes=[rf["f5"]])
                bank_free(b)
            dve(lambda e: e.tensor_tensor(f3[:, 0:n], f3[:, 0:n], cosT[:, 0:n], ALU.mult), reads=[rf["f3"], rf["cos"]], writes=[rf["f3"]])
            dve(lambda e: e.tensor_tensor(out_f[:, 0:n], f3[:, 0:n], f5[:, 0:n], ALU.add),
                reads=[rf["f3"], rf["f5"]], writes=[out_f_res])
            bn = n if b_n is None else b_n
            act(lambda e: e.activation(out_b[:, out_b_off:out_b_off + bn], out_f[:, b_src_off:b_src_off + bn], AF.Copy),
                reads=[out_f_res], writes=[out_b_res])

        Wi = wv(w_in)

        rope_tables(pos_prev, NP)
        norm_into_A(xT_prev, NP, T2, "g1")
        for j in range(8):
            pieces = wpieces(Wi, KC, AW + 256 * j)
            for f in range(2):
                hh = 2 * j + f
                banks = proj_A(pieces, f, A, rA[0:KC], 0, T2)
                qk_epilogue(banks, T2, P("kg"), NP, f2, rf["f2"], qb, rf["qb"], 0, 0, NP)
                dve(lambda e, hh=hh: e.tensor_reduce(kmean[:, hh, 0:4], f2[:, 0:NP].rearrange("p (b l) -> p b l", l=256), AX.X, ALU.add),
                    reads=[rf["f2"]], writes=[rf["kmean"]])
                S.dma("sp", ckv[0], k_dram[hh, :, :], qb[:, 0:NP], reads=[rf["qb"]], writes=[rf["kdram"]])
            pieces = wpieces(Wi, KC, 2 * AW + 256 * j)
            for t in range(8):
                b = bank_alloc()
                for (s, k0, kc) in pieces:
                    def fn(e, s=s, k0=k0, kc=kc, t=t, b=b):
                        inst = None
                        for kk in range(kc):
                            ki = k0 + kk
                            inst = e.matmul(ps[:, b, 0:256], A[:, ki, t * 128:(t + 1) * 128], ring[:, s, kk, :],
                                            start=(ki == 0), stop=(ki == KC - 1))
                        return inst
                    pe(fn, reads=[rslot[s]] + rA[k0:k0 + kc], writes=[rbank[b]])
                act(lambda e, t=t, b=b: e.activation(Vc[:, t, :], ps[:, b, 0:256], AF.Copy), reads=[rbank[b]], writes=[rf["Vc_prev"]])
                bank_free(b)
            S.dma("sp", ckv[1], v_dram[j, :, :, :], Vc[:, 0:8, :], reads=[rf["Vc_prev"]], writes=[rf["vdram"]])
        dve(lambda e: e.tensor_scalar(kmean[:, :, 0:4], kmean[:, :, 0:4], 1.0 / 256, None, ALU.mult),
            reads=[rf["kmean"]], writes=[rf["kmean"]])

        rope_tables(pos_ext, NT)
        norm_into_A(xT_ext, NT, TT3, "g1")
        if DBG:
            for k in range(KC):
                s = k % 2
                dve(lambda e, s=s, k=k: e.tensor_copy(xst[:, s, :], A[:, k, :]), reads=[rA[k]], writes=[rxst[s]])
                S.dma("sp", cxst[s], dbg["xn"][:, k, :], xst[:, s, :], reads=[rxst[s]])

        OWN0 = E
        filler_state["gen"] = gate_filler_gen()

        def attention(hl, hh):
            rdj = [rf["dj"], rf["dj2"]]
            rgs = [rf["gsm"], rf["gsm2"]]

            def gate_stage(qt):
                par = qt % 2
                g0 = 32 * par
                q0 = OWN0 + 128 * qt
                i = qt // 2
                ncand = 4 + i
                bg = bank_alloc()
                pe(lambda e, bg=bg, q0=q0: e.matmul(ps[:, bg, 0:8], f1[:, q0:q0 + 128], kmean[:, hh, :], start=True, stop=True),
                   reads=[rf["f1"], rf["kmean"]], writes=[rbank[bg]])
                dve(lambda e, bg=bg, i=i, g0=g0: e.tensor_tensor(gsm[:, g0:g0 + 8], ps[:, bg, 0:8], P("mask", 8 * i, 8), ALU.add),
                    reads=[rbank[bg], rf["prm"]], writes=[rgs[par]])
                bank_free(bg)
                dve(lambda e, g0=g0: e.max(gsm[:, g0 + 16:g0 + 24], gsm[:, g0:g0 + 11]), reads=[rgs[par]], writes=[rgs[par]])
                dve(lambda e, g0=g0: e.tensor_scalar(gsm[:, g0 + 24:g0 + 32], gsm[:, g0:g0 + 8], gsm[:, g0 + 18:g0 + 19], -BIG, ALU.is_lt, ALU.mult),
                    reads=[rgs[par]], writes=[rgs[par]])
                dve(lambda e, ncand=ncand, g0=g0, par=par: e.tensor_tensor(
                    djs[par][:, 0:ncand, :], ident_f.unsqueeze(1).broadcast_to([128, ncand, 128]),
                    gsm[:, g0 + 24:g0 + 24 + ncand].unsqueeze(2).broadcast_to([128, ncand, 128]), ALU.mult),
                    reads=[rgs[par], rf["cst"]], writes=[rdj[par]])

            for qt in range(-1, 8):
                if qt < 0:
                    q0, qn = 0, E
                    nkt = 8
                else:
                    q0, qn = OWN0 + 128 * qt, 128
                    nkt = 8 + qt + 1
                fill(2 if qt < 0 else 1)
                bo = bank_alloc()
                bs = bank_alloc()
                ngrp = (nkt + 3) // 4
                pend_pv = None
                for g in range(ngrp):
                    kts = list(range(4 * g, min(4 * g + 4, nkt)))
                    bS = bank_alloc()
                    pi = g % 2

                    def fnS(e, kts=kts, bS=bS, q0=q0, qn=qn, qt=qt):
                        inst = None
                        for gi, kt in enumerate(kts):
                            o = ps[:, bS, gi * 128:gi * 128 + qn]
                            if qt < 0:
                                mask = (ident_b, tri_b[:, 128 - E:128]) if kt == 7 else None
                            else:
                                blk = kt // 2
                                i = qt // 2
                                if blk < 4 + i:
                                    mask = (ones_b, djs[qt % 2][:, blk, :])
                                elif kt == 8 + qt:
                                    mask = (ident_b, tri_b)
                                else:
                                    mask = None
                            inst = e.matmul(o, Kc[:, hl, kt * 128:(kt + 1) * 128], qb[:, q0:q0 + qn], start=True, stop=(mask is None))
                            if mask is not None:
                                inst = e.matmul(o, mask[0], mask[1], start=False, stop=True)
                        return inst
                    pe(fnS, reads=[rKc[hl], rf["qb"], rdj[qt % 2], rf["cbf"]], writes=[rbank[bS]])
                    nk = len(kts)
                    if qn == 128:
                        act(lambda e, bS=bS, nk=nk, pi=pi: e.activation(pt[:, pi, 0:nk * 128], ps[:, bS, 0:nk * 128], AF.Exp, bias=negc, scale=1.0),
                            reads=[rbank[bS], rf["small"]], writes=[rpt[pi]])
                    else:
                        act(lambda e, bS=bS, nk=nk, pi=pi, qn=qn: e.activation(
                            pt[:, pi, 0:nk * 128].rearrange("p (g q) -> p g q", q=128)[:, :, 0:qn],
                            ps[:, bS, 0:nk * 128].rearrange("p (g q) -> p g q", q=128)[:, :, 0:qn], AF.Exp, bias=negc, scale=1.0),
                            reads=[rbank[bS], rf["small"]], writes=[rpt[pi]])
                    bank_free(bS)

                    def fnO(e, kts=kts, pi=pi, bo=bo, bs=bs, qn=qn, nkt=nkt):
                        inst = None
                        for gi, kt in enumerate(kts):
                            p_ap = pt[:, pi, gi * 128:gi * 128 + qn]
                            e.matmul(ps[:, bo, 0:qn], Vc[:, kt, hl * 128:(hl + 1) * 128], p_ap, start=(kt == 0), stop=(kt == nkt - 1))
                            inst = e.matmul(ps[:, bs, 0:qn], ones_b, p_ap, start=(kt == 0), stop=(kt == nkt - 1))
                        return inst
                    if pend_pv is not None:
                        pe(pend_pv[0], reads=pend_pv[1], writes=[rbank[bo], rbank[bs]])
                    pend_pv = (fnO, [rpt[pi], rf["Vc_prev"], rf["Vc_own"], rf["cbf"]])
                if qt + 1 <= 7:
                    gate_stage(qt + 1)
                fill(1)
                pe(pend_pv[0], reads=pend_pv[1], writes=[rbank[bo], rbank[bs]])
                ri = 0 if qt % 2 == 0 else 1
                dve(lambda e, bs=bs, ri=ri, qn=qn: e.tensor_scalar(rs[:, ri, 0:qn], ps[:, bs, 0:qn], 1e-30, None, ALU.max),
                    reads=[rbank[bs]], writes=[rrs[ri]])
                dve(lambda e, ri=ri, qn=qn: e.reciprocal(rs[:, ri, 0:qn], rs[:, ri, 0:qn]), reads=[rrs[ri]], writes=[rrs[ri]])
                dve(lambda e, bo=bo, ri=ri, q0=q0, qn=qn: e.tensor_tensor(BC[:, hh, q0:q0 + qn], ps[:, bo, 0:qn], rs[:, ri, 0:qn], ALU.mult),
                    reads=[rbank[bo], rrs[ri]], writes=[rBC[hh]])
                bank_free(bo)
                bank_free(bs)

        for j in range(8):
            for hl in range(2):
                S.dma("sp", ckvk[hl], Kc[:, hl, 0:NP], k_dram[2 * j + hl, :, :], reads=[rf["kdram"]], writes=[rKc[hl]])
            S.dma("sp", ckv[2], Vc[:, 0:8, :], v_dram[j, :, :, :], reads=[rf["vdram"]], writes=[rf["Vc_prev"]])
            pieces = wpieces(Wi, KC, 2 * AW + 256 * j)
            for t in range(8):
                b = bank_alloc()
                for (s, k0, kc) in pieces:
                    def fn(e, s=s, k0=k0, kc=kc, t=t, b=b):
                        inst = None
                        for kk in range(kc):
                            ki = k0 + kk
                            inst = e.matmul(ps[:, b, 0:256], A[:, ki, OWN0 + t * 128:OWN0 + (t + 1) * 128], ring[:, s, kk, :],
                                            start=(ki == 0), stop=(ki == KC - 1))
                        return inst
                    pe(fn, reads=[rslot[s]] + rA[k0:k0 + kc], writes=[rbank[b]])
                act(lambda e, t=t, b=b: e.activation(Vc[:, 8 + t, :], ps[:, b, 0:256], AF.Copy), reads=[rbank[b]], writes=[rf["Vc_own"]])
                bank_free(b)
            pieces = wpieces(Wi, KC, AW + 256 * j)
            for f in range(2):
                hh = 2 * j + f
                banks = proj_A(pieces, f, A, rA[0:KC], 0, TT3)
                qk_epilogue(banks, TT3, P("kg"), NT, f2, rf["f2"], Kc[:, f, :], rKc[f], NP, OWN0, NO)
                dve(lambda e, hh=hh: e.tensor_reduce(kmean[:, hh, 4:8], f2[:, OWN0:NT].rearrange("p (b l) -> p b l", l=256), AX.X, ALU.add),
                    reads=[rf["f2"]], writes=[rf["kmean"]])
                dve(lambda e, hh=hh: e.tensor_scalar(kmean[:, hh, 4:8], kmean[:, hh, 4:8], 1.0 / 256, None, ALU.mult),
                    reads=[rf["kmean"]], writes=[rf["kmean"]])
            pieces = wpieces(Wi, KC, 256 * j)
            for f in range(2):
                hh = 2 * j + f
                banks = proj_A(pieces, f, A, rA[0:KC], 0, TT3)
                qk_epilogue(banks, TT3, qgs, NT, f1, rf["f1"], qb, rf["qb"], 0, 0, NT)
                attention(f, hh)
        if DBG:
            for k in range(16):
                s = k % 2
                dve(lambda e, s=s, k=k: e.tensor_copy(xst[:, s, :], BC[:, k, :]), reads=[rBC[k]], writes=[rxst[s]])
                S.dma("sp", cxst[s], dbg["attn"][:, k, :], xst[:, s, :], reads=[rxst[s]])

        sgb = [f0, f2]
        rsgb = [rf["f0"], rf["f2"]]
        hgl = [(f1, rf["f1"]), (cosT, rf["cos"])]
        m3 = {"pa": None, "pb": None}

        def m3_bproj(j, ff):
            if ff == 0:
                m3["pb"] = wpieces(Wi, KC, 3 * AW + CD + 256 * j)
            bb = proj_A(m3["pb"], ff, A, rA[0:KC], 0, TT3)
            for bi, (t0, tn) in enumerate(TT3):
                act(lambda e, bi=bi, t0=t0, tn=tn, bb=bb, ff=ff: e.activation(sgb[ff][:, t0:t0 + tn], ps[:, bb[bi], 0:tn], AF.Sigmoid),
                    reads=[rbank[bb[bi]]], writes=[rsgb[ff]])
                bank_free(bb[bi])

        def m3_aproj(ch):
            j, f = ch // 2, ch % 2
            if f == 0:
                m3["pa"] = wpieces(Wi, KC, 3 * AW + 256 * j)
            ba = proj_A(m3["pa"], f, A, rA[0:KC], 0, TT3)
            hb, hres = hgl[ch % 2]
            for bi, (t0, tn) in enumerate(TT3):
                dve(lambda e, bi=bi, t0=t0, tn=tn, ba=ba, f=f, hb=hb: e.tensor_tensor(hb[:, t0:t0 + tn], ps[:, ba[bi], 0:tn], sgb[f][:, t0:t0 + tn], ALU.mult),
                    reads=[rbank[ba[bi]], rsgb[f]], writes=[hres])
                bank_free(ba[bi])

        def m3_stage(ch):
            j, f = ch // 2, ch % 2
            if ch == 0:
                m3_bproj(0, 0)
                m3_bproj(0, 1)
                m3_aproj(0)
            elif f == 0:
                m3_bproj(j, 1)
                m3_aproj(ch)
            else:
                m3_aproj(ch)
                if j + 1 < 8:
                    m3_bproj(j + 1, 0)

        m3_stage(0)
        for ch in range(16):
            if ch + 1 < 16:
                m3_stage(ch + 1)
            hb, hres = hgl[ch % 2]
            cs = sinT
            rcs = rf["sin"]
            dve(lambda e, ch=ch, hb=hb: e.tensor_scalar(xst[:, 0, 0:NF], hb[:, 0:NF], P("convw", ch * 31), P("convb", ch), ALU.mult, ALU.add),
                reads=[hres, rf["prm"]], writes=[rxst[0]])
            dve(lambda e, ch=ch, hb=hb: e.tensor_scalar(xst[:, 1, 0:NF], hb[:, 1:1 + NF], P("convw", ch * 31 + 1), None, ALU.mult),
                reads=[hres, rf["prm"]], writes=[rxst[1]])
            for tap in range(2, CW):
                a_ = tap % 2
                dve(lambda e, ch=ch, a_=a_, tap=tap, hb=hb: e.scalar_tensor_tensor(xst[:, a_, 0:NF], hb[:, tap:tap + NF], P("convw", ch * 31 + tap),
                                                                                   xst[:, a_, 0:NF], ALU.mult, ALU.add),
                    reads=[hres, rxst[a_]], writes=[rxst[a_]])
            dve(lambda e: e.tensor_tensor(cs[:, 0:NF], xst[:, 0, 0:NF], xst[:, 1, 0:NF], ALU.add),
                reads=[rxst[0], rxst[1]], writes=[rcs])
            s = ch % 2
            S.dma("sp", cspill[s], c_dram[ch, :, :], cs[:, 0:NF], reads=[rcs], writes=[rf["cdram"]])
            if ch == 0:
                dve(lambda e: e.tensor_copy(f3[:, 0:NF], cs[:, 0:NF]), reads=[rcs], writes=[rf["f3"]])
                act(lambda e: e.activation(f4[:, 0:NF], cs[:, 0:NF], AF.Square), reads=[rcs], writes=[rf["f4"]])
            else:
                dve(lambda e: e.tensor_tensor(f3[:, 0:NF], f3[:, 0:NF], cs[:, 0:NF], ALU.add),
                    reads=[rcs, rf["f3"]], writes=[rf["f3"]])
                act(lambda e: e.activation(f5[:, 0:NF], cs[:, 0:NF], AF.Square), reads=[rcs], writes=[rf["f5"]])
                dve(lambda e: e.tensor_tensor(f4[:, 0:NF], f4[:, 0:NF], f5[:, 0:NF], ALU.add),
                    reads=[rf["f5"], rf["f4"]], writes=[rf["f4"]])
        for (t0, tn) in TF3:
            b = bank_alloc()
            pe(lambda e, b=b, t0=t0, tn=tn: e.matmul(ps[:, b, 0:tn], onesf[:, :], f3[:, t0:t0 + tn], start=True, stop=True),
               reads=[rf["f3"], rf["onesf"]], writes=[rbank[b]])
            act(lambda e, b=b, t0=t0, tn=tn: e.activation(f0[:, t0:t0 + tn], ps[:, b, 0:tn], AF.Identity, scale=1.0 / CD),
                reads=[rbank[b]], writes=[rf["f0"]])
            bank_free(b)
            b = bank_alloc()
            pe(lambda e, b=b, t0=t0, tn=tn: e.matmul(ps[:, b, 0:tn], onesf[:, :], f4[:, t0:t0 + tn], start=True, stop=True),
               reads=[rf["f4"], rf["onesf"]], writes=[rbank[b]])
            dve(lambda e, t0=t0, tn=tn: e.tensor_tensor(f1[:, t0:t0 + tn], f0[:, t0:t0 + tn], f0[:, t0:t0 + tn], ALU.mult),
                reads=[rf["f0"]], writes=[rf["f1"]])
            dve(lambda e, b=b, t0=t0, tn=tn: e.scalar_tensor_tensor(f2[:, t0:t0 + tn], ps[:, b, 0:tn], 1.0 / CD, f1[:, t0:t0 + tn], ALU.mult, ALU.subtract),
                reads=[rbank[b], rf["f1"]], writes=[rf["f2"]])
            bank_free(b)
        gate_stage[0] = (f3, rf["f3"], cgl[0])
        gate_stage[1] = (f4, rf["f4"], cgl[1])
        act(lambda e: e.activation(f2[:, 0:NF], f2[:, 0:NF], AF.Ln, bias=eps_ap, scale=1.0), reads=[rf["f2"], rf["small"]], writes=[rf["f2"]])
        act(lambda e: e.activation(f2[:, 0:NF], f2[:, 0:NF], AF.Exp, scale=-0.5), reads=[rf["f2"]], writes=[rf["f2"]])
        alias_toks = []
        for nm in ["cos", "sin", "Kc0", "Kc1", "Vc_prev", "Vc_own", "pt0", "pt1", "dj", "dj2"]:
            r_ = rf[nm]
            alias_toks.append(r_.w)
            for key, (sem_, val_, teng_) in r_.r.items():
                alias_toks.append((key, sem_, val_, teng_))
        S.wait_all("act", [S.lasttok["pe"], S.lasttok["dve"]] + S.lastdma[-8:] + alias_toks)
        S.wait_all("dve", [S.lasttok["pe"], S.lasttok["act"]] + S.lastdma[-8:] + alias_toks)
        for ch in range(16):
            s = ch % 2
            S.dma("sp", cxst[s], xst[:, s, 0:NF], c_dram[ch, :, :], reads=[rf["cdram"]], writes=[rxst[s]])
            dve(lambda e, s=s: e.tensor_tensor(xst[:, s, 0:NF], xst[:, s, 0:NF], f0[:, 0:NF], ALU.subtract),
                reads=[rxst[s], rf["f0"]], writes=[rxst[s]])
            dve(lambda e, s=s: e.tensor_tensor(xst[:, s, 0:NF], xst[:, s, 0:NF], f2[:, 0:NF], ALU.mult),
                reads=[rxst[s], rf["f2"]], writes=[rxst[s]])
            act(lambda e, s=s, ch=ch: e.activation(BC[:, 16 + ch, E - 2:NT], xst[:, s, 0:NF], AF.Silu, bias=P("lnb", ch), scale=P("lng", ch)),
                reads=[rxst[s], rf["prm"]], writes=[rBC[16 + ch]])
            fill(6)
        fill(100000)
        dve(lambda e: e.memset(BC[:, 16:32, 0:E - 2], 0.0), writes=rBC[16:32])
        if DBG:
            for k in range(16):
                s = k % 2
                dve(lambda e, s=s, k=k: e.tensor_copy(xst[:, s, :], BC[:, 16 + k, :]), reads=[rBC[16 + k]], writes=[rxst[s]])
                S.dma("sp", cxst[s], dbg["hc"][:, k, :], xst[:, s, :], reads=[rxst[s]])

        Woa = wv(w_oa)
        Woc = wv(w_oc)
        for j in range(16):
            pa = wpieces(Woa, 16, 256 * j)
            pc = wpieces(Woc, 16, 256 * j)
            for f in range(2):
                c = 2 * j + f
                S.dma("sp", cxst[0], xst[:, 0, :], g_dram[c, :, :], reads=[rf["gdram"]], writes=[rxst[0]])
                S.dma("sp", cxst[1], xst[:, 1, :], g_dram[32 + c, :, :], reads=[rf["gdram"]], writes=[rxst[1]])
                ba = proj_A(pa, f, BC, rBC[0:16], 0, TT3)
                for bi, (t0, tn) in enumerate(TT3):
                    dve(lambda e, bi=bi, t0=t0, tn=tn, ba=ba: e.tensor_tensor(f0[:, t0:t0 + tn], ps[:, ba[bi], 0:tn], xst[:, 0, t0:t0 + tn], ALU.mult),
                        reads=[rbank[ba[bi]], rxst[0]], writes=[rf["f0"]])
                    bank_free(ba[bi])
                bc_ = proj_A(pc, f, BC, rBC[16:32], 16, TT3)
                for bi, (t0, tn) in enumerate(TT3):
                    dve(lambda e, bi=bi, t0=t0, tn=tn, bc_=bc_: e.tensor_tensor(f1[:, t0:t0 + tn], ps[:, bc_[bi], 0:tn], xst[:, 1, t0:t0 + tn], ALU.mult),
                        reads=[rbank[bc_[bi]], rxst[1]], writes=[rf["f1"]])
                    bank_free(bc_[bi])
                dve(lambda e, c=c: e.tensor_tensor(A[:, c, :], f0[:, :], f1[:, :], ALU.add),
                    reads=[rf["f0"], rf["f1"]], writes=[rA[c]])
        if DBG:
            for k in range(32):
                s = k % 2
                dve(lambda e, s=s, k=k: e.tensor_copy(xst[:, s, :], A[:, k, :]), reads=[rA[k]], writes=[rxst[s]])
                S.dma("sp", cxst[s], dbg["z"][:, k, :], xst[:, s, :], reads=[rxst[s]])

        Wo = wv(w_out)
        H0 = E - 2
        for j in range(16):
            pieces = wpieces(Wo, KC, 256 * j)
            for f in range(2):
                c = 2 * j + f
                s = c % 2
                S.dma("sp", cxst[s], xst[:, s, :], xT_ext[c * 128:(c + 1) * 128, :], writes=[rxst[s]])
                banks = proj_A(pieces, f, A, rA[0:KC], 0, TT3)
                for bi, (t0, tn) in enumerate(TT3):
                    dve(lambda e, bi=bi, t0=t0, tn=tn, banks=banks, s=s: e.tensor_tensor(xst[:, s, t0:t0 + tn], ps[:, banks[bi], 0:tn], xst[:, s, t0:t0 + tn], ALU.add),
                        reads=[rbank[banks[bi]], rxst[s]], writes=[rxst[s]])
                    bank_free(banks[bi])
                all_out_toks.append(S.dma("sp", cout[s], outT[c * 128:(c + 1) * 128, :], xst[:, s, E:NT], reads=[rxst[s]]))
                dve(lambda e, s=s, c=c: e.tensor_scalar(BC[:, c, 0:NF], xst[:, s, H0:NT], P("g2", c), None, ALU.mult),
                    reads=[rxst[s], rf["prm"]], writes=[rBC[c]])
                if c == 0:
                    act(lambda e, s=s: e.activation(f3[:, 0:NF], xst[:, s, H0:NT], AF.Square), reads=[rxst[s]], writes=[rf["f3"]])
                else:
                    act(lambda e, s=s: e.activation(f5[:, 0:NF], xst[:, s, H0:NT], AF.Square), reads=[rxst[s]], writes=[rf["f5"]])
                    dve(lambda e: e.tensor_tensor(f3[:, 0:NF], f3[:, 0:NF], f5[:, 0:NF], ALU.add),
                        reads=[rf["f5"], rf["f3"]], writes=[rf["f3"]])
        colsum_to(f4, rf["f4"], f3, rf["f3"], TF3, 1.0 / D, rstd_chain(f4, rf["f4"], 1.0 / D, EPS))
        dve(lambda e: e.tensor_scalar(f4[:, 0:2], f4[:, 0:2], P("flag"), None, ALU.mult), reads=[rf["f4"], rf["prm"]], writes=[rf["f4"]])

        Wfi = wv(w_fi)
        Wfo = wv(w_fo)
        NPAIR = DFF // 256
        groups = [(0, 11), (11, 11), (22, 11), (33, 10)]
        sg = [f0, f1]
        rsg = [rf["f0"], rf["f1"]]

        def ffn_conv(banks, fch, dst, dres):
            for bi, (t0, tn) in enumerate(TF3):
                dve(lambda e, bi=bi, t0=t0, tn=tn: e.tensor_tensor(f2[:, t0:t0 + tn], ps[:, banks[bi], 0:tn], f4[:, t0:t0 + tn], ALU.mult),
                    reads=[rbank[banks[bi]], rf["f4"]], writes=[rf["f2"]])
                bank_free(banks[bi])
            dve(lambda e: e.tensor_scalar(dst[:, 0:NO], f2[:, 0:NO], P("fcw", fch * 3), P("fcb", fch), ALU.mult, ALU.add),
                reads=[rf["f2"], rf["prm"]], writes=[dres])
            for tap in (1, 2):
                dve(lambda e, tap=tap: e.scalar_tensor_tensor(dst[:, 0:NO], f2[:, tap:tap + NO], P("fcw", fch * 3 + tap), dst[:, 0:NO], ALU.mult, ALU.add),
                    reads=[rf["f2"], dres, rf["prm"]], writes=[dres])

        pending = []
        for (p0, npair) in groups:
            ng = 2 * npair
            for jj in range(npair):
                j = p0 + jj
                pg = wpieces(Wfi, KC, 256 * j)
                for f in range(2):
                    banks = proj_A(pg, f, BC, rBC[0:KC], 0, TF3)
                    ffn_conv(banks, 2 * j + f, sg[f], rsg[f])
                    act(lambda e, f=f: e.activation(sg[f][:, 0:NO], sg[f][:, 0:NO], AF.Silu), reads=[rsg[f]], writes=[rsg[f]])
                pu = wpieces(Wfi, KC, DFF + 256 * j)
                for f in range(2):
                    banks = proj_A(pu, f, BC, rBC[0:KC], 0, TF3)
                    ffn_conv(banks, 86 + 2 * j + f, f3, rf["f3"])
                    ai = 2 * jj + f
                    dve(lambda e, f=f, ai=ai: e.tensor_tensor(A[:, ai, 0:NO], sg[f][:, 0:NO], f3[:, 0:NO], ALU.mult),
                        reads=[rsg[f], rf["f3"]], writes=[rA[ai]])
            for jo in range(16):
                pcs = []
                k0 = 0
                while k0 < ng:
                    kc = min(KPIECE, ng - k0)
                    pcs.append((wload(Wfo, 2 * p0 + k0, kc, 256 * jo), k0, kc))
                    k0 += kc
                for (tokp) in pending:
                    tokp()
                pending = []
                for f in range(2):
                    c = 2 * jo + f
                    s = c % 2
                    banks = proj_A(pcs, f, A, rA[0:ng], 0, T2)
                    for bi, (t0, tn) in enumerate(T2):
                        act(lambda e, bi=bi, t0=t0, tn=tn, banks=banks, s=s: e.activation(xst[:, s, t0:t0 + tn], ps[:, banks[bi], 0:tn], AF.Copy),
                            reads=[rbank[banks[bi]]], writes=[rxst[s]])
                        bank_free(banks[bi])

                    def mk(c=c, s=s):
                        def go():
                            all_out_toks.append(S.dma("pool", cout[s], outT[c * 128:(c + 1) * 128, :], xst[:, s, 0:NO],
                                                      reads=[rxst[s]], accum_op=ALU.add))
                        return go
                    pending.append(mk())
        for tokp in pending:
            tokp()
        S.wait_all("sp", all_out_toks[-4:])
        S.wait_all("pool", all_out_toks[-4:])

        with nc.Block() as block:
            @block.tensor
            def _(e):
                S.emit("pe", e)

            @block.scalar
            def _(e):
                S.emit("act", e)

            @block.vector
            def _(e):
                S.emit("dve", e)

            @block.gpsimd
            def _(e):
                S.emit("pool", e)

            @block.sync
            def _(e):
                S.emit("sp", e)
    return nc


def _lay(v, nchunk):
    return np.ascontiguousarray(np.asarray(v, np.float32).reshape(nchunk, 128).T)


_NC_CACHE = {}


def kernel(x, positions, norm1_g, w_in, q_norm_g, k_norm_g, w_o_attn, conv_w, conv_b, conv_ln_g, conv_ln_b,
           w_o_conv, w_out, norm2_g, w_ffn_in, ffn_conv_w, ffn_conv_b, w_ffn_out):
    x = np.asarray(x, np.float32)
    positions = np.asarray(positions, np.int32)
    B, Sq, _ = x.shape
    l = 0
    ident = np.eye(128, dtype=np.float32)
    rot = np.zeros((128, 128), np.float32)
    for m in range(64):
        rot[m + 64, m] = -1.0
        rot[m, m + 64] = 1.0
    tri = np.where(np.arange(128)[:, None] <= np.arange(128)[None, :], 0.0, -BIG).astype(np.float32)
    cst = np.ascontiguousarray(np.concatenate([ident, rot, tri], axis=1))
    half = HD // 2
    inv_freq = (np.float32(10000.0) ** (-np.arange(half, dtype=np.float32) / np.float32(half))).astype(np.float32)
    invf = np.concatenate([inv_freq, inv_freq]).reshape(128, 1)

    prm_base = np.zeros((128, NPRM), np.float32)

    def put(name, arr):
        arr = np.asarray(arr, np.float32)
        prm_base[:, _cols[name]:_cols[name] + arr.shape[1]] = arr

    put("g1", _lay(norm1_g[l], 32))
    put("g2", _lay(norm2_g[l], 32))
    put("qg", np.asarray(q_norm_g[l], np.float32).reshape(128, 1))
    put("kg", np.asarray(k_norm_g[l], np.float32).reshape(128, 1))
    cw = np.asarray(conv_w[l], np.float32)
    put("convw", np.ascontiguousarray(cw.T.reshape(16, 128, 31).transpose(1, 0, 2)).reshape(128, 16 * 31))
    put("convb", _lay(conv_b[l], 16))
    put("lng", _lay(conv_ln_g[l], 16))
    put("lnb", _lay(conv_ln_b[l], 16))
    fw = np.asarray(ffn_conv_w[l], np.float32)
    put("fcw", np.ascontiguousarray(fw.T.reshape(172, 128, 3).transpose(1, 0, 2)).reshape(128, 172 * 3))
    put("fcb", _lay(ffn_conv_b[l], 172))
    put("invf", invf)
    rowqk = np.concatenate([np.asarray(q_norm_g[l], np.float32), np.asarray(k_norm_g[l], np.float32)]).reshape(1, 256)

    ws = {
        "w_in": np.ascontiguousarray(np.asarray(w_in[l], np.float32)),
        "w_o_attn": np.ascontiguousarray(np.asarray(w_o_attn[l], np.float32)),
        "w_o_conv": np.ascontiguousarray(np.asarray(w_o_conv[l], np.float32)),
        "w_out": np.ascontiguousarray(np.asarray(w_out[l], np.float32)),
        "w_ffn_in": np.ascontiguousarray(np.asarray(w_ffn_in[l], np.float32)),
        "w_ffn_out": np.ascontiguousarray(np.asarray(w_ffn_out[l], np.float32)),
    }
    in_maps = []
    for c in range(8):
        b, h = c // 2, c % 2
        s = h * NO
        xT = x[b].T
        pos = positions[b]
        xT_ext = np.zeros((D, NT), np.float32)
        pos_e = np.zeros((NT,), np.int32)
        xT_prev = np.zeros((D, NP), np.float32)
        pos_p = np.zeros((NP,), np.int32)
        if h == 0:
            xT_ext[:, E:] = xT[:, 0:NO]
            pos_e[E:] = pos[0:NO]
        else:
            xT_ext[:, :] = xT[:, s - E:s + NO]
            pos_e[:] = pos[s - E:s + NO]
            xT_prev[:, :] = xT[:, s - NP:s]
            pos_p[:] = pos[s - NP:s]
        prm = prm_base.copy()
        prm[:, _cols["flag"]] = float(h)
        mask = np.zeros((4, 8), np.float32)
        for i in range(4):
            for jn in range(8):
                if jn < 4:
                    mask[i, jn] = 0.0 if h == 1 else -BIG
                else:
                    mask[i, jn] = 0.0 if (jn - 4) < i else -BIG
        prm[:, _cols["mask"]:_cols["mask"] + 32] = mask.reshape(1, 32)
        m = {
            "xT_ext": xT_ext, "xT_prev": xT_prev,
            "pos_ext": np.ascontiguousarray(np.broadcast_to(pos_e[None, :], (128, NT))),
            "pos_prev": np.ascontiguousarray(np.broadcast_to(pos_p[None, :], (128, NP))),
            "prm": prm, "cst": cst, "rowqk": rowqk,
        }
        m.update(ws)
        in_maps.append(m)
    if "nc" not in _NC_CACHE:
        _NC_CACHE["nc"] = build_nc()
    nc = _NC_CACHE["nc"]
    res = run_bass_kernel_spmd(nc, in_maps, core_ids=list(range(8)))
    out = np.empty((B, Sq, D), np.float32)
    for c in range(8):
        b, h = c // 2, c % 2
        out[b, h * NO:(h + 1) * NO, :] = res.results[c]["outT"].T
    if DBG:
        kernel.dbg = res.results
    return out
```

```python
import os
import numpy as np
import concourse.bass as bass
import concourse.mybir as mybir
from concourse.bass_utils import run_bass_kernel_spmd

F32 = mybir.dt.float32
BF16 = mybir.dt.bfloat16
I32 = mybir.dt.int32
AF = mybir.ActivationFunctionType
ALU = mybir.AluOpType
AX = mybir.AxisListType

D = 4096
NH = 16
HD = 128
AW = 2048
CD = 2048
CW = 31
DFF = 11008
INW = 18432
EPS = 1e-6
E = 32
NO = 1024
NP = 1024
NT = NO + E
NF = NO + 2
BIG = 30000.0
TT3 = [(0, 352), (352, 352), (704, 352)]
TF3 = [(0, 342), (342, 342), (684, 342)]
T2 = [(0, 512), (512, 512)]
KC = 32
NSLOT = 12
KPIECE = 4
TWO_PI = 6.283185307179586
C1 = 6.28125
C2 = TWO_PI - C1

_cols = {}
_off = 0
for _n, _w in [("g1", 32), ("g2", 32), ("qg", 1), ("kg", 1), ("convw", 16 * 31), ("convb", 16),
               ("lng", 16), ("lnb", 16), ("fcw", 172 * 3), ("fcb", 172), ("invf", 1), ("flag", 1),
               ("mask", 32)]:
    _cols[_n] = _off
    _off += _w
NPRM = _off

DBG = os.environ.get("MK_DBG", "")


class Res:
    __slots__ = ("w", "ws", "r")

    def __init__(self):
        self.w = None
        self.ws = {}
        self.r = {}


class Chan:
    def __init__(self, sem):
        self.sem = sem
        self.n = 0


class Sched:
    ENGS = ("pe", "act", "dve", "pool", "sp")

    def __init__(self, sems):
        self.sem = sems
        self.cnt = {e: 0 for e in self.ENGS}
        self.ops = {e: [] for e in self.ENGS}
        self.seen = {e: {} for e in self.ENGS}
        self.lasttok = {e: None for e in self.ENGS}
        self.lastdma = []

    def _waits(self, eng, need):
        out = []
        best = {}
        for tok in need:
            if tok is None:
                continue
            key, sem, val, teng = tok
            if teng == "pe" and eng == "pe":
                continue
            if best.get(key, (None, 0))[1] < val:
                best[key] = (sem, val)
        seen = self.seen[eng]
        for key, (sem, val) in best.items():
            if seen.get(key, 0) < val:
                seen[key] = val
                out.append((sem, val))
        return out

    def _deps(self, reads, writes):
        need = []
        for r in reads:
            need.extend(r.ws.values())
        for w in writes:
            need.extend(w.ws.values())
            for key, (sem, val, teng) in w.r.items():
                need.append((key, sem, val, teng))
        return need

    def _commit(self, tok, reads, writes):
        key, sem, val, teng = tok
        for r in reads:
            cur = r.r.get(key)
            if cur is None or cur[1] < val:
                r.r[key] = (sem, val, teng)
        for w in writes:
            w.w = tok
            if teng == "dma":
                w.ws = {k: v for k, v in w.ws.items() if v[3] == "dma" and k != key}
            else:
                w.ws = {}
            w.ws[key] = tok
            w.r = {}

    def op(self, eng, fn, reads=(), writes=()):
        waits = self._waits(eng, self._deps(reads, writes))
        self.cnt[eng] += 1
        tok = ("e_" + eng, self.sem[eng], self.cnt[eng], eng)
        self.ops[eng].append((waits, fn, (self.sem[eng], 1)))
        self._commit(tok, reads, writes)
        self.lasttok[eng] = tok
        return tok

    def dma(self, eng, chan, out, in_, reads=(), writes=(), **kw):
        waits = self._waits(eng, self._deps(reads, writes))
        chan.n += 16
        tok = ("c_%d" % id(chan), chan.sem, chan.n, "dma")
        self.ops[eng].append((waits, lambda e, o=out, i=in_, k=kw: e.dma_start(out=o, in_=i, **k), (chan.sem, 16)))
        self._commit(tok, reads, writes)
        self.lastdma.append(tok)
        return tok

    def wait_all(self, eng, toks):
        waits = self._waits(eng, toks)
        self.ops[eng].append((waits, None, None))

    def emit(self, eng, e):
        for waits, fn, inc in self.ops[eng]:
            for sem, val in waits:
                e.wait_ge(sem, val)
            if fn is not None:
                inst = fn(e)
                inst.then_inc(inc[0], inc[1])


def build_nc():
    nc = bass.Bass("TRN2", target_bir_lowering=False)
    dt = nc.dram_tensor
    xT_ext = dt("xT_ext", [D, NT], F32, kind="ExternalInput").ap()
    xT_prev = dt("xT_prev", [D, NP], F32, kind="ExternalInput").ap()
    pos_ext = dt("pos_ext", [128, NT], I32, kind="ExternalInput").ap()
    pos_prev = dt("pos_prev", [128, NP], I32, kind="ExternalInput").ap()
    prm_d = dt("prm", [128, NPRM], F32, kind="ExternalInput").ap()
    cst_d = dt("cst", [128, 3 * 128], F32, kind="ExternalInput").ap()
    row_d = dt("rowqk", [1, 256], F32, kind="ExternalInput").ap()
    w_in = dt("w_in", [D, INW], F32, kind="ExternalInput").ap()
    w_oa = dt("w_o_attn", [AW, D], F32, kind="ExternalInput").ap()
    w_oc = dt("w_o_conv", [CD, D], F32, kind="ExternalInput").ap()
    w_out = dt("w_out", [D, D], F32, kind="ExternalInput").ap()
    w_fi = dt("w_ffn_in", [D, 2 * DFF], F32, kind="ExternalInput").ap()
    w_fo = dt("w_ffn_out", [DFF, D], F32, kind="ExternalInput").ap()
    outT = dt("outT", [D, NO], F32, kind="ExternalOutput").ap()
    k_dram = dt("k_scr", [NH, 128, NP], BF16, kind="Internal").ap()
    v_dram = dt("v_scr", [8, 128, 8, 256], BF16, kind="Internal").ap()
    c_dram = dt("c_scr", [16, 128, NF], F32, kind="Internal").ap()
    g_dram = dt("g_scr", [64, 128, NT], F32, kind="Internal").ap()
    dbg = {}
    if DBG:
        dbg["attn"] = dt("dbg_attn", [128, 16, NT], F32, kind="ExternalOutput").ap()
        dbg["hc"] = dt("dbg_hc", [128, 16, NT], F32, kind="ExternalOutput").ap()
        dbg["z"] = dt("dbg_z", [128, 32, NT], F32, kind="ExternalOutput").ap()
        dbg["xn"] = dt("dbg_xn", [128, 32, NT], F32, kind="ExternalOutput").ap()

    def wv(W):
        return W.rearrange("(k p) c -> p k c", p=128)

    from contextlib import ExitStack
    with ExitStack() as es:
        def sb(name, shape, dtype):
            return es.enter_context(nc.sbuf_tensor(name, shape, dtype))

        A = sb("A", [128, 32, NT], BF16)
        BC = sb("BC", [128, 32, NT], BF16)
        ring = sb("ring", [128, NSLOT, KPIECE, 256], BF16)
        NFS = 4
        fring = sb("fring", [128, NFS, KPIECE, 128], BF16)
        prm = sb("prm_s", [128, NPRM], F32)
        cst = sb("cst_s", [128, 3 * 128], F32)
        cbf = sb("cbf_s", [128, 3 * 128], BF16)
        onesf = sb("onesf", [128, 128], F32)
        row = sb("row_s", [1, 264], F32)
        small = sb("small_s", [128, 64], F32)
        kmean = sb("kmean", [128, NH, 8], F32)
        xst = sb("xst", [128, 2, NT], F32)
        f0 = sb("f0", [128, NT], F32)
        f1 = sb("f1", [128, NT], F32)
        f2 = sb("f2", [128, NT], F32)
        f3 = sb("f3", [128, NT], F32)
        f4 = sb("f4", [128, NT], F32)
        f5 = sb("f5", [128, NT], F32)
        qb = sb("qb", [128, NT], BF16)
        flat = BC[:, 16:32, :].rearrange("p a b -> p (a b)")
        cosT = flat[:, 0:2 * NT].bitcast(F32)
        sinT = flat[:, 2 * NT:4 * NT].bitcast(F32)
        o_ = 4 * NT
        Kc = flat[:, o_:o_ + 4096].rearrange("p (h t) -> p h t", h=2)
        Vc = flat[:, o_ + 4096:o_ + 8192].rearrange("p (t c) -> p t c", c=256)
        pt = flat[:, o_ + 8192:o_ + 9216].rearrange("p (a b) -> p a b", a=2)
        dj = flat[:, o_ + 9216:o_ + 10240].rearrange("p (a b) -> p a b", a=8)
        dj2 = flat[:, o_ + 10240:o_ + 11264].rearrange("p (a b) -> p a b", a=8)
        djs = [dj, dj2]
        gsm = sb("gsm", [128, 64], F32)
        rs = sb("rs", [128, 2, 128], F32)
        ps = es.enter_context(nc.psum_tensor("ps", [128, 8, 512], F32))
        sem_names = ["pe", "act", "dve", "pool", "sp"]
        sems = {n: es.enter_context(nc.semaphore("s_" + n)) for n in sem_names}
        chan_sems = [es.enter_context(nc.semaphore("c_%d" % i)) for i in range(70)]
        chan_i = [0]

        def newchan():
            c = Chan(chan_sems[chan_i[0]])
            chan_i[0] += 1
            return c

        S = Sched(sems)

        rA = [Res() for _ in range(32)]
        rBC = [Res() for _ in range(32)]
        rslot = [Res() for _ in range(NSLOT)]
        cslot = [newchan() for _ in range(NSLOT)]
        rxst = [Res(), Res()]
        rfs = [Res() for _ in range(4)]
        cfs = [newchan() for _ in range(4)]
        cxst = [newchan(), newchan()]
        rbank = [Res() for _ in range(8)]
        bank_open = [False] * 8
        bank_ptr = [0]
        rf = {n: Res() for n in ["f0", "f1", "f2", "f3", "f4", "f5", "qb", "cos", "sin", "prm", "cst", "cbf",
                                 "onesf", "row", "small", "kmean", "pt0", "pt1", "dj", "dj2", "gsm", "gsm2", "rs0", "rs1",
                                 "Kc0", "Kc1", "Vc_prev", "Vc_own", "kdram", "vdram", "cdram", "gdram", "out"]}
        rKc = [rf["Kc0"], rf["Kc1"]]
        rpt = [rf["pt0"], rf["pt1"]]
        rrs = [rf["rs0"], rf["rs1"]]
        cmisc = newchan()
        ckv = [newchan(), newchan(), newchan()]
        ckvk = [newchan(), newchan()]
        cout = [newchan(), newchan()]
        cspill = [newchan(), newchan()]
        cdbg = newchan()
        cnorm = [newchan(), newchan()]
        cgl = [newchan(), newchan()]
        all_out_toks = []

        bank_stamp = [0] * 8
        stamp_ctr = [0]

        def bank_alloc():
            best = None
            for b in range(8):
                if not bank_open[b] and (best is None or bank_stamp[b] < bank_stamp[best]):
                    best = b
            if best is None:
                raise RuntimeError("no psum bank")
            bank_open[best] = True
            return best

        def bank_free(b):
            bank_open[b] = False
            stamp_ctr[0] += 1
            bank_stamp[b] = stamp_ctr[0]

        slot_ptr = [0]

        def wload(Wv, k0, kc, c0):
            s = slot_ptr[0]
            slot_ptr[0] = (s + 1) % NSLOT
            S.dma("pool", cslot[s], ring[:, s, 0:kc, :], Wv[:, k0:k0 + kc, c0:c0 + 256], writes=[rslot[s]])
            return s

        def wpieces(Wv, kchunks, c0):
            out = []
            k0 = 0
            assert (kchunks + KPIECE - 1) // KPIECE <= NSLOT
            while k0 < kchunks:
                kc = min(KPIECE, kchunks - k0)
                out.append((wload(Wv, k0, kc, c0), k0, kc))
                k0 += kc
            return out

        def mm_group(out_ap, pairs, reads, bank):
            def fn(pe, out_ap=out_ap, pairs=pairs):
                n = len(pairs)
                inst = None
                for i, (l, r) in enumerate(pairs):
                    inst = pe.matmul(out_ap, l, r, start=(i == 0), stop=(i == n - 1))
                return inst
            return S.op("pe", fn, reads=reads, writes=[rbank[bank]])

        def proj_A(pieces, f, X, xres, kbase, tts):
            bl = [bank_alloc() for _ in tts]
            nk = sum(kc for (_, _, kc) in pieces)
            kdone = 0
            for pi_, (s, k0, kc) in enumerate(pieces):
                def fn(pe, s=s, k0=k0, kc=kc, kdone=kdone, f=f, X=X, kbase=kbase, tts=tts, bl=bl, nk=nk):
                    inst = None
                    for kk in range(kc):
                        ki = kdone + kk
                        for ti, (t0, tn) in enumerate(tts):
                            inst = pe.matmul(ps[:, bl[ti], 0:tn], ring[:, s, kk, f * 128:(f + 1) * 128],
                                             X[:, kbase + k0 + kk, t0:t0 + tn], start=(ki == 0), stop=(ki == nk - 1))
                    return inst
                S.op("pe", fn, reads=[rslot[s]] + xres[k0:k0 + kc], writes=[rbank[b] for b in bl])
                kdone += kc
            return bl

        GATE0 = 3 * AW + 2 * CD
        NKP = KC // KPIECE

        def gate_filler_gen():
            units = [(gch, p) for gch in range(64) for p in range(NKP)]
            loaded = {}

            def load(ui):
                if ui < len(units) and ui not in loaded:
                    gch, p = units[ui]
                    sl = ui % NFS
                    S.dma("pool", cfs[sl], fring[:, sl, :, :],
                          wv(w_in)[:, p * KPIECE:(p + 1) * KPIECE, GATE0 + 128 * gch:GATE0 + 128 * (gch + 1)], writes=[rfs[sl]])
                    loaded[ui] = sl
            bl = None
            for ui, (gch, p) in enumerate(units):
                for a_ in range(NFS - 1):
                    load(ui + a_)
                if p == 0:
                    bl = [bank_alloc() for _ in TT3]
                sl = loaded[ui]

                def fn(pe, sl=sl, p=p, bl=bl):
                    inst = None
                    for kk in range(KPIECE):
                        ki = p * KPIECE + kk
                        for ti, (t0, tn) in enumerate(TT3):
                            inst = pe.matmul(ps[:, bl[ti], 0:tn], fring[:, sl, kk, :], A[:, ki, t0:t0 + tn],
                                             start=(ki == 0), stop=(ki == KC - 1))
                    return inst
                S.op("pe", fn, reads=[rfs[sl]] + rA[p * KPIECE:(p + 1) * KPIECE], writes=[rbank[b] for b in bl])
                if p == NKP - 1:
                    gbuf, gres, gchan = gate_stage[gch % 2]
                    for bi, (t0, tn) in enumerate(TT3):
                        act(lambda e, bi=bi, t0=t0, tn=tn, bl=bl, gbuf=gbuf: e.activation(gbuf[:, t0:t0 + tn], ps[:, bl[bi], 0:tn], AF.Sigmoid),
                            reads=[rbank[bl[bi]]], writes=[gres])
                        bank_free(bl[bi])
                    S.dma("sp", gchan, g_dram[gch, :, :], gbuf[:, :], reads=[gres], writes=[rf["gdram"]])
                yield

        gate_stage = [(xst[:, 0, :], rxst[0], cspill[0]), (xst[:, 1, :], rxst[1], cspill[1])]
        filler_state = {"gen": None}

        def fill(n):
            g = filler_state["gen"]
            if g is None:
                return
            for _ in range(n):
                try:
                    next(g)
                except StopIteration:
                    filler_state["gen"] = None
                    return

        def dve(fn, reads=(), writes=()):
            return S.op("dve", fn, reads, writes)

        def act(fn, reads=(), writes=()):
            return S.op("act", fn, reads, writes)

        def pe(fn, reads=(), writes=()):
            return S.op("pe", fn, reads, writes)

        P = lambda n, i=0, w=1: prm[:, _cols[n] + i:_cols[n] + i + w]

        S.dma("sp", newchan(), prm[:, :], prm_d[:, :], writes=[rf["prm"]])
        S.dma("sp", newchan(), cst[:, :], cst_d[:, :], writes=[rf["cst"]])
        S.dma("sp", newchan(), row[:, 0:256], row_d[:, :], writes=[rf["row"]])
        ident_f = cst[:, 0:128]
        rot_f = cst[:, 128:256]
        tri_f = cst[:, 256:384]
        dve(lambda e: e.memset(onesf[:, :], 1.0), writes=[rf["onesf"]])
        dve(lambda e: e.tensor_copy(cbf[:, 0:128], cst[:, 0:128]), reads=[rf["cst"]], writes=[rf["cbf"]])
        dve(lambda e: e.memset(cbf[:, 128:256], 1.0), writes=[rf["cbf"]])
        dve(lambda e: e.tensor_copy(cbf[:, 256:384], cst[:, 256:384]), reads=[rf["cst"]], writes=[rf["cbf"]])
        ident_b = cbf[:, 0:128]
        ones_b = cbf[:, 128:256]
        tri_b = cbf[:, 256:384]
        dve(lambda e: e.tensor_reduce(row[:, 256:257], row[:, 0:128], AX.X, ALU.max, apply_absolute_value=True),
            reads=[rf["row"]], writes=[rf["row"]])
        dve(lambda e: e.tensor_reduce(row[:, 257:258], row[:, 128:256], AX.X, ALU.max, apply_absolute_value=True),
            reads=[rf["row"]], writes=[rf["row"]])
        dve(lambda e: e.tensor_tensor(row[:, 258:259], row[:, 256:257], row[:, 257:258], ALU.mult),
            reads=[rf["row"]], writes=[rf["row"]])
        b0 = bank_alloc()
        pe(lambda e: e.matmul(ps[:, b0, 0:2], onesf[0:1, :], row[0:1, 258:260], start=True, stop=True),
           reads=[rf["row"], rf["onesf"]], writes=[rbank[b0]])
        negc = small[:, 0:1]
        qgs = small[:, 1:2]
        dve(lambda e: e.tensor_scalar(negc, ps[:, b0, 0:1], -(128.0 ** 0.5), None, ALU.mult),
            reads=[rbank[b0]], writes=[rf["small"]])
        bank_free(b0)
        dve(lambda e: e.tensor_scalar(qgs, P("qg"), 128.0 ** -0.5, None, ALU.mult), reads=[rf["prm"]], writes=[rf["small"]])

        def rope_tables(pos_d, n):
            S.dma("sp", cxst[0], xst[:, 0, 0:n].bitcast(I32), pos_d[:, :], writes=[rxst[0]])
            dve(lambda e: e.tensor_copy(f0[:, 0:n], xst[:, 0, 0:n].bitcast(I32)), reads=[rxst[0]], writes=[rf["f0"]])
            dve(lambda e: e.tensor_scalar(f1[:, 0:n], f0[:, 0:n], P("invf"), None, ALU.mult),
                reads=[rf["f0"], rf["prm"]], writes=[rf["f1"]])
            dve(lambda e: e.tensor_scalar(f2[:, 0:n], f1[:, 0:n], 1.0 / TWO_PI, None, ALU.mult),
                reads=[rf["f1"]], writes=[rf["f2"]])
            dve(lambda e: e.tensor_copy(xst[:, 1, 0:n].bitcast(I32), f2[:, 0:n]), reads=[rf["f2"]], writes=[rxst[1]])
            dve(lambda e: e.tensor_copy(f2[:, 0:n], xst[:, 1, 0:n].bitcast(I32)), reads=[rxst[1]], writes=[rf["f2"]])
            dve(lambda e: e.scalar_tensor_tensor(f1[:, 0:n], f2[:, 0:n], -C1, f1[:, 0:n], ALU.mult, ALU.add),
                reads=[rf["f2"], rf["f1"]], writes=[rf["f1"]])
            dve(lambda e: e.scalar_tensor_tensor(f1[:, 0:n], f2[:, 0:n], -C2, f1[:, 0:n], ALU.mult, ALU.add),
                reads=[rf["f2"], rf["f1"]], writes=[rf["f1"]])
            PI = float(np.pi)

            def wrap(dst, dres, shift):
                dve(lambda e: e.tensor_scalar(dst[:, 0:n], f1[:, 0:n], shift, None, ALU.add), reads=[rf["f1"]], writes=[dres])
                dve(lambda e: e.tensor_scalar(f3[:, 0:n], dst[:, 0:n], PI, -TWO_PI, ALU.is_gt, ALU.mult), reads=[dres], writes=[rf["f3"]])
                dve(lambda e: e.tensor_tensor(dst[:, 0:n], dst[:, 0:n], f3[:, 0:n], ALU.add), reads=[dres, rf["f3"]], writes=[dres])
                dve(lambda e: e.tensor_scalar(f3[:, 0:n], dst[:, 0:n], -PI, TWO_PI, ALU.is_lt, ALU.mult), reads=[dres], writes=[rf["f3"]])
                dve(lambda e: e.tensor_tensor(dst[:, 0:n], dst[:, 0:n], f3[:, 0:n], ALU.add), reads=[dres, rf["f3"]], writes=[dres])
            wrap(f0, rf["f0"], 0.0)
            wrap(f2, rf["f2"], PI / 2)
            act(lambda e: e.activation(sinT[:, 0:n], f0[:, 0:n], AF.Sin), reads=[rf["f0"]], writes=[rf["sin"]])
            act(lambda e: e.activation(cosT[:, 0:n], f2[:, 0:n], AF.Sin), reads=[rf["f2"]], writes=[rf["cos"]])

        def colsum_to(dst, dres, src, sres, tts, scale, func_chain):
            for (t0, tn) in tts:
                b = bank_alloc()
                pe(lambda e, b=b, t0=t0, tn=tn: e.matmul(ps[:, b, 0:tn], onesf[:, :], src[:, t0:t0 + tn], start=True, stop=True),
                   reads=[sres, rf["onesf"]], writes=[rbank[b]])
                func_chain(b, t0, tn)
                bank_free(b)

        def rstd_chain(dst, dres, scale, eps):
            def chain(b, t0, tn):
                act(lambda e: e.activation(dst[:, t0:t0 + tn], ps[:, b, 0:tn], AF.Ln, bias=eps_ap, scale=scale),
                    reads=[rbank[b], rf["small"]], writes=[dres])
                act(lambda e: e.activation(dst[:, t0:t0 + tn], dst[:, t0:t0 + tn], AF.Exp, scale=-0.5),
                    reads=[dres], writes=[dres])
            return chain

        eps_ap = small[:, 2:3]
        dve(lambda e: e.memset(eps_ap, EPS), writes=[rf["small"]])
        dve(lambda e: e.memset(gsm[:, 8:11], -BIG / 2), writes=[rf["gsm"]])
        dve(lambda e: e.memset(gsm[:, 40:43], -BIG / 2), writes=[rf["gsm2"]])

        def norm_into_A(xT_d, n, tts, gname, dbg_name=None):
            stg = [(xst[:, 0, :], rxst[0], cxst[0]), (xst[:, 1, :], rxst[1], cxst[1]),
                   (f0, rf["f0"], cnorm[0]), (f1, rf["f1"], cnorm[1])]
            sqt = [(f5, rf["f5"]), (f2, rf["f2"]), (f3, rf["f3"])]
            nbk = [bank_alloc() for _ in tts]
            for k in range(KC):
                buf, res, ch = stg[k % 4]
                S.dma("sp", ch, buf[:, 0:n], xT_d[k * 128:(k + 1) * 128, :], writes=[res])
                tq, rq = sqt[k % 3]
                act(lambda e, buf=buf, tq=tq: e.activation(tq[:, 0:n], buf[:, 0:n], AF.Square), reads=[res], writes=[rq])

                def fn(e, tq=tq, k=k):
                    inst = None
                    for ti, (t0, tn) in enumerate(tts):
                        inst = e.matmul(ps[:, nbk[ti], 0:tn], onesf[:, :], tq[:, t0:t0 + tn], start=(k == 0), stop=(k == KC - 1))
                    return inst
                pe(fn, reads=[rq, rf["onesf"]], writes=[rbank[b] for b in nbk])
            chain = rstd_chain(f4, rf["f4"], 1.0 / D, EPS)
            for ti, (t0, tn) in enumerate(tts):
                chain(nbk[ti], t0, tn)
                bank_free(nbk[ti])
            for k in range(KC):
                buf, res, ch = stg[k % 4]
                S.dma("sp", ch, buf[:, 0:n], xT_d[k * 128:(k + 1) * 128, :], writes=[res])
                dve(lambda e, buf=buf, k=k: e.scalar_tensor_tensor(A[:, k, 0:n], buf[:, 0:n], P(gname, k), f4[:, 0:n], ALU.mult, ALU.mult),
                    reads=[res, rf["f4"], rf["prm"]], writes=[rA[k]])

        def qk_epilogue(banks, tts, gcol, n, out_f, out_f_res, out_b, out_b_res, out_b_off=0, b_src_off=0, b_n=None):
            for bi, (t0, tn) in enumerate(tts):
                act(lambda e, bi=bi, t0=t0, tn=tn: e.activation(f5[:, t0:t0 + tn], ps[:, banks[bi], 0:tn], AF.Square),
                    reads=[rbank[banks[bi]]], writes=[rf["f5"]])
            fill(1)
            for (t0, tn) in tts:
                b = bank_alloc()
                pe(lambda e, b=b, t0=t0, tn=tn: e.matmul(ps[:, b, 0:tn], onesf[:, :], f5[:, t0:t0 + tn], start=True, stop=True),
                   reads=[rf["f5"], rf["onesf"]], writes=[rbank[b]])
                act(lambda e, b=b, t0=t0, tn=tn: e.activation(f4[:, t0:t0 + tn], ps[:, b, 0:tn], AF.Ln, bias=eps_ap, scale=1.0 / HD),
                    reads=[rbank[b], rf["small"]], writes=[rf["f4"]])
                bank_free(b)
            act(lambda e: e.activation(f4[:, 0:n], f4[:, 0:n], AF.Exp, scale=-0.5), reads=[rf["f4"]], writes=[rf["f4"]])
            for bi, (t0, tn) in enumerate(tts):
                dve(lambda e, bi=bi, t0=t0, tn=tn: e.scalar_tensor_tensor(f3[:, t0:t0 + tn], ps[:, banks[bi], 0:tn], gcol,
                                                                          f4[:, t0:t0 + tn], ALU.mult, ALU.mult),
                    reads=[rbank[banks[bi]], rf["f4"], rf["small"], rf["prm"]], writes=[rf["f3"]])
                bank_free(banks[bi])
            fill(3)
            for (t0, tn) in tts:
                b = bank_alloc()
                pe(lambda e, b=b, t0=t0, tn=tn: e.matmul(ps[:, b, 0:tn], rot_f, f3[:, t0:t0 + tn], start=True, stop=True),
                   reads=[rf["f3"], rf["cst"]], writes=[rbank[b]])
                dve(lambda e, b=b, t0=t0, tn=tn: e.tensor_tensor(f5[:, t0:t0 + tn], ps[:, b, 0:tn], sinT[:, t0:t0 + tn], ALU.mult),
                    reads=[rbank[b], rf["sin"]], writes=[rf["f5"]])
                bank_free(b)
            dve(lambda e: e.tensor_tensor(f3[:, 0:n], f3[:, 0:n], cosT[:, 0:n], ALU.mult), reads=[rf["f3"], rf["cos"]], writes=[rf["f3"]])
            dve(lambda e: e.tensor_tensor(out_f[:, 0:n], f3[:, 0:n], f5[:, 0:n], ALU.add),
                reads=[rf["f3"], rf["f5"]], writes=[out_f_res])
            bn = n if b_n is None else b_n
            act(lambda e: e.activation(out_b[:, out_b_off:out_b_off + bn], out_f[:, b_src_off:b_src_off + bn], AF.Copy),
                reads=[out_f_res], writes=[out_b_res])

        Wi = wv(w_in)

        rope_tables(pos_prev, NP)
        norm_into_A(xT_prev, NP, T2, "g1")
        for j in range(8):
            pieces = wpieces(Wi, KC, AW + 256 * j)
            for f in range(2):
                hh = 2 * j + f
                banks = proj_A(pieces, f, A, rA[0:KC], 0, T2)
                qk_epilogue(banks, T2, P("kg"), NP, f2, rf["f2"], qb, rf["qb"], 0, 0, NP)
                dve(lambda e, hh=hh: e.tensor_reduce(kmean[:, hh, 0:4], f2[:, 0:NP].rearrange("p (b l) -> p b l", l=256), AX.X, ALU.add),
                    reads=[rf["f2"]], writes=[rf["kmean"]])
                S.dma("sp", ckv[0], k_dram[hh, :, :], qb[:, 0:NP], reads=[rf["qb"]], writes=[rf["kdram"]])
            pieces = wpieces(Wi, KC, 2 * AW + 256 * j)
            for t in range(8):
                b = bank_alloc()
                for (s, k0, kc) in pieces:
                    def fn(e, s=s, k0=k0, kc=kc, t=t, b=b):
                        inst = None
                        for kk in range(kc):
                            ki = k0 + kk
                            inst = e.matmul(ps[:, b, 0:256], A[:, ki, t * 128:(t + 1) * 128], ring[:, s, kk, :],
                                            start=(ki == 0), stop=(ki == KC - 1))
                        return inst
                    pe(fn, reads=[rslot[s]] + rA[k0:k0 + kc], writes=[rbank[b]])
                act(lambda e, t=t, b=b: e.activation(Vc[:, t, :], ps[:, b, 0:256], AF.Copy), reads=[rbank[b]], writes=[rf["Vc_prev"]])
                bank_free(b)
            S.dma("sp", ckv[1], v_dram[j, :, :, :], Vc[:, 0:8, :], reads=[rf["Vc_prev"]], writes=[rf["vdram"]])
        dve(lambda e: e.tensor_scalar(kmean[:, :, 0:4], kmean[:, :, 0:4], 1.0 / 256, None, ALU.mult),
            reads=[rf["kmean"]], writes=[rf["kmean"]])

        rope_tables(pos_ext, NT)
        norm_into_A(xT_ext, NT, TT3, "g1")
        if DBG:
            for k in range(KC):
                s = k % 2
                dve(lambda e, s=s, k=k: e.tensor_copy(xst[:, s, :], A[:, k, :]), reads=[rA[k]], writes=[rxst[s]])
                S.dma("sp", cxst[s], dbg["xn"][:, k, :], xst[:, s, :], reads=[rxst[s]])

        OWN0 = E
        filler_state["gen"] = gate_filler_gen()

        def attention(hl, hh):
            rdj = [rf["dj"], rf["dj2"]]
            rgs = [rf["gsm"], rf["gsm2"]]

            def gate_stage(qt):
                par = qt % 2
                g0 = 32 * par
                q0 = OWN0 + 128 * qt
                i = qt // 2
                ncand = 4 + i
                bg = bank_alloc()
                pe(lambda e, bg=bg, q0=q0: e.matmul(ps[:, bg, 0:8], f1[:, q0:q0 + 128], kmean[:, hh, :], start=True, stop=True),
                   reads=[rf["f1"], rf["kmean"]], writes=[rbank[bg]])
                dve(lambda e, bg=bg, i=i, g0=g0: e.tensor_tensor(gsm[:, g0:g0 + 8], ps[:, bg, 0:8], P("mask", 8 * i, 8), ALU.add),
                    reads=[rbank[bg], rf["prm"]], writes=[rgs[par]])
                bank_free(bg)
                dve(lambda e, g0=g0: e.max(gsm[:, g0 + 16:g0 + 24], gsm[:, g0:g0 + 11]), reads=[rgs[par]], writes=[rgs[par]])
                dve(lambda e, g0=g0: e.tensor_scalar(gsm[:, g0 + 24:g0 + 32], gsm[:, g0:g0 + 8], gsm[:, g0 + 18:g0 + 19], -BIG, ALU.is_lt, ALU.mult),
                    reads=[rgs[par]], writes=[rgs[par]])
                dve(lambda e, ncand=ncand, g0=g0, par=par: e.tensor_tensor(
                    djs[par][:, 0:ncand, :], ident_f.unsqueeze(1).broadcast_to([128, ncand, 128]),
                    gsm[:, g0 + 24:g0 + 24 + ncand].unsqueeze(2).broadcast_to([128, ncand, 128]), ALU.mult),
                    reads=[rgs[par], rf["cst"]], writes=[rdj[par]])

            for qt in range(-1, 8):
                if qt < 0:
                    q0, qn = 0, E
                    nkt = 8
                else:
                    q0, qn = OWN0 + 128 * qt, 128
                    nkt = 8 + qt + 1
                fill(2 if qt < 0 else 1)
                bo = bank_alloc()
                bs = bank_alloc()
                ngrp = (nkt + 3) // 4
                pend_pv = None
                for g in range(ngrp):
                    kts = list(range(4 * g, min(4 * g + 4, nkt)))
                    bS = bank_alloc()
                    pi = g % 2

                    def fnS(e, kts=kts, bS=bS, q0=q0, qn=qn, qt=qt):
                        inst = None
                        for gi, kt in enumerate(kts):
                            o = ps[:, bS, gi * 128:gi * 128 + qn]
                            if qt < 0:
                                mask = (ident_b, tri_b[:, 128 - E:128]) if kt == 7 else None
                            else:
                                blk = kt // 2
                                i = qt // 2
                                if blk < 4 + i:
                                    mask = (ones_b, djs[qt % 2][:, blk, :])
                                elif kt == 8 + qt:
                                    mask = (ident_b, tri_b)
                                else:
                                    mask = None
                            inst = e.matmul(o, Kc[:, hl, kt * 128:(kt + 1) * 128], qb[:, q0:q0 + qn], start=True, stop=(mask is None))
                            if mask is not None:
                                inst = e.matmul(o, mask[0], mask[1], start=False, stop=True)
                        return inst
                    pe(fnS, reads=[rKc[hl], rf["qb"], rdj[qt % 2], rf["cbf"]], writes=[rbank[bS]])
                    nk = len(kts)
                    if qn == 128:
                        act(lambda e, bS=bS, nk=nk, pi=pi: e.activation(pt[:, pi, 0:nk * 128], ps[:, bS, 0:nk * 128], AF.Exp, bias=negc, scale=1.0),
                            reads=[rbank[bS], rf["small"]], writes=[rpt[pi]])
                    else:
                        act(lambda e, bS=bS, nk=nk, pi=pi, qn=qn: e.activation(
                            pt[:, pi, 0:nk * 128].rearrange("p (g q) -> p g q", q=128)[:, :, 0:qn],
                            ps[:, bS, 0:nk * 128].rearrange("p (g q) -> p g q", q=128)[:, :, 0:qn], AF.Exp, bias=negc, scale=1.0),
                            reads=[rbank[bS], rf["small"]], writes=[rpt[pi]])
                    bank_free(bS)

                    def fnO(e, kts=kts, pi=pi, bo=bo, bs=bs, qn=qn, nkt=nkt):
                        inst = None
                        for gi, kt in enumerate(kts):
                            p_ap = pt[:, pi, gi * 128:gi * 128 + qn]
                            e.matmul(ps[:, bo, 0:qn], Vc[:, kt, hl * 128:(hl + 1) * 128], p_ap, start=(kt == 0), stop=(kt == nkt - 1))
                            inst = e.matmul(ps[:, bs, 0:qn], ones_b, p_ap, start=(kt == 0), stop=(kt == nkt - 1))
                        return inst
                    if pend_pv is not None:
                        pe(pend_pv[0], reads=pend_pv[1], writes=[rbank[bo], rbank[bs]])
                    pend_pv = (fnO, [rpt[pi], rf["Vc_prev"], rf["Vc_own"], rf["cbf"]])
                if qt + 1 <= 7:
                    gate_stage(qt + 1)
                fill(1)
                pe(pend_pv[0], reads=pend_pv[1], writes=[rbank[bo], rbank[bs]])
                ri = 0 if qt % 2 == 0 else 1
                dve(lambda e, bs=bs, ri=ri, qn=qn: e.tensor_scalar(rs[:, ri, 0:qn], ps[:, bs, 0:qn], 1e-30, None, ALU.max),
                    reads=[rbank[bs]], writes=[rrs[ri]])
                dve(lambda e, ri=ri, qn=qn: e.reciprocal(rs[:, ri, 0:qn], rs[:, ri, 0:qn]), reads=[rrs[ri]], writes=[rrs[ri]])
                dve(lambda e, bo=bo, ri=ri, q0=q0, qn=qn: e.tensor_tensor(BC[:, hh, q0:q0 + qn], ps[:, bo, 0:qn], rs[:, ri, 0:qn], ALU.mult),
                    reads=[rbank[bo], rrs[ri]], writes=[rBC[hh]])
                bank_free(bo)
                bank_free(bs)

        for j in range(8):
            for hl in range(2):
                S.dma("sp", ckvk[hl], Kc[:, hl, 0:NP], k_dram[2 * j + hl, :, :], reads=[rf["kdram"]], writes=[rKc[hl]])
            S.dma("sp", ckv[2], Vc[:, 0:8, :], v_dram[j, :, :, :], reads=[rf["vdram"]], writes=[rf["Vc_prev"]])
            pieces = wpieces(Wi, KC, 2 * AW + 256 * j)
            for t in range(8):
                b = bank_alloc()
                for (s, k0, kc) in pieces:
                    def fn(e, s=s, k0=k0, kc=kc, t=t, b=b):
                        inst = None
                        for kk in range(kc):
                            ki = k0 + kk
                            inst = e.matmul(ps[:, b, 0:256], A[:, ki, OWN0 + t * 128:OWN0 + (t + 1) * 128], ring[:, s, kk, :],
                                            start=(ki == 0), stop=(ki == KC - 1))
                        return inst
                    pe(fn, reads=[rslot[s]] + rA[k0:k0 + kc], writes=[rbank[b]])
                act(lambda e, t=t, b=b: e.activation(Vc[:, 8 + t, :], ps[:, b, 0:256], AF.Copy), reads=[rbank[b]], writes=[rf["Vc_own"]])
                bank_free(b)
            pieces = wpieces(Wi, KC, AW + 256 * j)
            for f in range(2):
                hh = 2 * j + f
                banks = proj_A(pieces, f, A, rA[0:KC], 0, TT3)
                qk_epilogue(banks, TT3, P("kg"), NT, f2, rf["f2"], Kc[:, f, :], rKc[f], NP, OWN0, NO)
                dve(lambda e, hh=hh: e.tensor_reduce(kmean[:, hh, 4:8], f2[:, OWN0:NT].rearrange("p (b l) -> p b l", l=256), AX.X, ALU.add),
                    reads=[rf["f2"]], writes=[rf["kmean"]])
                dve(lambda e, hh=hh: e.tensor_scalar(kmean[:, hh, 4:8], kmean[:, hh, 4:8], 1.0 / 256, None, ALU.mult),
                    reads=[rf["kmean"]], writes=[rf["kmean"]])
            pieces = wpieces(Wi, KC, 256 * j)
            for f in range(2):
                hh = 2 * j + f
                banks = proj_A(pieces, f, A, rA[0:KC], 0, TT3)
                qk_epilogue(banks, TT3, qgs, NT, f1, rf["f1"], qb, rf["qb"], 0, 0, NT)
                attention(f, hh)
        if DBG:
            for k in range(16):
                s = k % 2
                dve(lambda e, s=s, k=k: e.tensor_copy(xst[:, s, :], BC[:, k, :]), reads=[rBC[k]], writes=[rxst[s]])
                S.dma("sp", cxst[s], dbg["attn"][:, k, :], xst[:, s, :], reads=[rxst[s]])

        sgb = [f0, f2]
        rsgb = [rf["f0"], rf["f2"]]
        hgl = [(f1, rf["f1"]), (cosT, rf["cos"])]
        m3 = {"pa": None, "pb": None}

        def m3_bproj(j, ff):
            if ff == 0:
                m3["pb"] = wpieces(Wi, KC, 3 * AW + CD + 256 * j)
            bb = proj_A(m3["pb"], ff, A, rA[0:KC], 0, TT3)
            for bi, (t0, tn) in enumerate(TT3):
                act(lambda e, bi=bi, t0=t0, tn=tn, bb=bb, ff=ff: e.activation(sgb[ff][:, t0:t0 + tn], ps[:, bb[bi], 0:tn], AF.Sigmoid),
                    reads=[rbank[bb[bi]]], writes=[rsgb[ff]])
                bank_free(bb[bi])

        def m3_aproj(ch):
            j, f = ch // 2, ch % 2
            if f == 0:
                m3["pa"] = wpieces(Wi, KC, 3 * AW + 256 * j)
            ba = proj_A(m3["pa"], f, A, rA[0:KC], 0, TT3)
            hb, hres = hgl[ch % 2]
            for bi, (t0, tn) in enumerate(TT3):
                dve(lambda e, bi=bi, t0=t0, tn=tn, ba=ba, f=f, hb=hb: e.tensor_tensor(hb[:, t0:t0 + tn], ps[:, ba[bi], 0:tn], sgb[f][:, t0:t0 + tn], ALU.mult),
                    reads=[rbank[ba[bi]], rsgb[f]], writes=[hres])
                bank_free(ba[bi])

        def m3_stage(ch):
            j, f = ch // 2, ch % 2
            if ch == 0:
                m3_bproj(0, 0)
                m3_bproj(0, 1)
                m3_aproj(0)
            elif f == 0:
                m3_bproj(j, 1)
                m3_aproj(ch)
            else:
                m3_aproj(ch)
                if j + 1 < 8:
                    m3_bproj(j + 1, 0)

        m3_stage(0)
        for ch in range(16):
            if ch + 1 < 16:
                m3_stage(ch + 1)
            hb, hres = hgl[ch % 2]
            cs = sinT
            rcs = rf["sin"]
            dve(lambda e, ch=ch, hb=hb: e.tensor_scalar(xst[:, 0, 0:NF], hb[:, 0:NF], P("convw", ch * 31), P("convb", ch), ALU.mult, ALU.add),
                reads=[hres, rf["prm"]], writes=[rxst[0]])
            dve(lambda e, ch=ch, hb=hb: e.tensor_scalar(xst[:, 1, 0:NF], hb[:, 1:1 + NF], P("convw", ch * 31 + 1), None, ALU.mult),
                reads=[hres, rf["prm"]], writes=[rxst[1]])
            for tap in range(2, CW):
                a_ = tap % 2
                dve(lambda e, ch=ch, a_=a_, tap=tap, hb=hb: e.scalar_tensor_tensor(xst[:, a_, 0:NF], hb[:, tap:tap + NF], P("convw", ch * 31 + tap),
                                                                                   xst[:, a_, 0:NF], ALU.mult, ALU.add),
                    reads=[hres, rxst[a_]], writes=[rxst[a_]])
            dve(lambda e: e.tensor_tensor(cs[:, 0:NF], xst[:, 0, 0:NF], xst[:, 1, 0:NF], ALU.add),
                reads=[rxst[0], rxst[1]], writes=[rcs])
            s = ch % 2
            S.dma("sp", cspill[s], c_dram[ch, :, :], cs[:, 0:NF], reads=[rcs], writes=[rf["cdram"]])
            if ch == 0:
                dve(lambda e: e.tensor_copy(f3[:, 0:NF], cs[:, 0:NF]), reads=[rcs], writes=[rf["f3"]])
                act(lambda e: e.activation(f4[:, 0:NF], cs[:, 0:NF], AF.Square), reads=[rcs], writes=[rf["f4"]])
            else:
                dve(lambda e: e.tensor_tensor(f3[:, 0:NF], f3[:, 0:NF], cs[:, 0:NF], ALU.add),
                    reads=[rcs, rf["f3"]], writes=[rf["f3"]])
                act(lambda e: e.activation(f5[:, 0:NF], cs[:, 0:NF], AF.Square), reads=[rcs], writes=[rf["f5"]])
                dve(lambda e: e.tensor_tensor(f4[:, 0:NF], f4[:, 0:NF], f5[:, 0:NF], ALU.add),
                    reads=[rf["f5"], rf["f4"]], writes=[rf["f4"]])
        for (t0, tn) in TF3:
            b = bank_alloc()
            pe(lambda e, b=b, t0=t0, tn=tn: e.matmul(ps[:, b, 0:tn], onesf[:, :], f3[:, t0:t0 + tn], start=True, stop=True),
               reads=[rf["f3"], rf["onesf"]], writes=[rbank[b]])
            act(lambda e, b=b, t0=t0, tn=tn: e.activation(f0[:, t0:t0 + tn], ps[:, b, 0:tn], AF.Identity, scale=1.0 / CD),
                reads=[rbank[b]], writes=[rf["f0"]])
            bank_free(b)
            b = bank_alloc()
            pe(lambda e, b=b, t0=t0, tn=tn: e.matmul(ps[:, b, 0:tn], onesf[:, :], f4[:, t0:t0 + tn], start=True, stop=True),
               reads=[rf["f4"], rf["onesf"]], writes=[rbank[b]])
            dve(lambda e, t0=t0, tn=tn: e.tensor_tensor(f1[:, t0:t0 + tn], f0[:, t0:t0 + tn], f0[:, t0:t0 + tn], ALU.mult),
                reads=[rf["f0"]], writes=[rf["f1"]])
            dve(lambda e, b=b, t0=t0, tn=tn: e.scalar_tensor_tensor(f2[:, t0:t0 + tn], ps[:, b, 0:tn], 1.0 / CD, f1[:, t0:t0 + tn], ALU.mult, ALU.subtract),
                reads=[rbank[b], rf["f1"]], writes=[rf["f2"]])
            bank_free(b)
        gate_stage[0] = (f3, rf["f3"], cgl[0])
        gate_stage[1] = (f4, rf["f4"], cgl[1])
        act(lambda e: e.activation(f2[:, 0:NF], f2[:, 0:NF], AF.Ln, bias=eps_ap, scale=1.0), reads=[rf["f2"], rf["small"]], writes=[rf["f2"]])
        act(lambda e: e.activation(f2[:, 0:NF], f2[:, 0:NF], AF.Exp, scale=-0.5), reads=[rf["f2"]], writes=[rf["f2"]])
        alias_toks = []
        for nm in ["cos", "sin", "Kc0", "Kc1", "Vc_prev", "Vc_own", "pt0", "pt1", "dj", "dj2"]:
            r_ = rf[nm]
            alias_toks.append(r_.w)
            for key, (sem_, val_, teng_) in r_.r.items():
                alias_toks.append((key, sem_, val_, teng_))
        S.wait_all("act", [S.lasttok["pe"], S.lasttok["dve"]] + S.lastdma[-8:] + alias_toks)
        S.wait_all("dve", [S.lasttok["pe"], S.lasttok["act"]] + S.lastdma[-8:] + alias_toks)
        for ch in range(16):
            s = ch % 2
            S.dma("sp", cxst[s], xst[:, s, 0:NF], c_dram[ch, :, :], reads=[rf["cdram"]], writes=[rxst[s]])
            dve(lambda e, s=s: e.tensor_tensor(xst[:, s, 0:NF], xst[:, s, 0:NF], f0[:, 0:NF], ALU.subtract),
                reads=[rxst[s], rf["f0"]], writes=[rxst[s]])
            dve(lambda e, s=s: e.tensor_tensor(xst[:, s, 0:NF], xst[:, s, 0:NF], f2[:, 0:NF], ALU.mult),
                reads=[rxst[s], rf["f2"]], writes=[rxst[s]])
            act(lambda e, s=s, ch=ch: e.activation(BC[:, 16 + ch, E - 2:NT], xst[:, s, 0:NF], AF.Silu, bias=P("lnb", ch), scale=P("lng", ch)),
                reads=[rxst[s], rf["prm"]], writes=[rBC[16 + ch]])
            fill(6)
        fill(100000)
        dve(lambda e: e.memset(BC[:, 16:32, 0:E - 2], 0.0), writes=rBC[16:32])
        if DBG:
            for k in range(16):
                s = k % 2
                dve(lambda e, s=s, k=k: e.tensor_copy(xst[:, s, :], BC[:, 16 + k, :]), reads=[rBC[16 + k]], writes=[rxst[s]])
                S.dma("sp", cxst[s], dbg["hc"][:, k, :], xst[:, s, :], reads=[rxst[s]])

        Woa = wv(w_oa)
        Woc = wv(w_oc)
        for j in range(16):
            pa = wpieces(Woa, 16, 256 * j)
            pc = wpieces(Woc, 16, 256 * j)
            for f in range(2):
                c = 2 * j + f
                S.dma("sp", cxst[0], xst[:, 0, :], g_dram[c, :, :], reads=[rf["gdram"]], writes=[rxst[0]])
                S.dma("sp", cxst[1], xst[:, 1, :], g_dram[32 + c, :, :], reads=[rf["gdram"]], writes=[rxst[1]])
                ba = proj_A(pa, f, BC, rBC[0:16], 0, TT3)
                for bi, (t0, tn) in enumerate(TT3):
                    dve(lambda e, bi=bi, t0=t0, tn=tn, ba=ba: e.tensor_tensor(f0[:, t0:t0 + tn], ps[:, ba[bi], 0:tn], xst[:, 0, t0:t0 + tn], ALU.mult),
                        reads=[rbank[ba[bi]], rxst[0]], writes=[rf["f0"]])
                    bank_free(ba[bi])
                bc_ = proj_A(pc, f, BC, rBC[16:32], 16, TT3)
                for bi, (t0, tn) in enumerate(TT3):
                    dve(lambda e, bi=bi, t0=t0, tn=tn, bc_=bc_: e.tensor_tensor(f1[:, t0:t0 + tn], ps[:, bc_[bi], 0:tn], xst[:, 1, t0:t0 + tn], ALU.mult),
                        reads=[rbank[bc_[bi]], rxst[1]], writes=[rf["f1"]])
                    bank_free(bc_[bi])
                dve(lambda e, c=c: e.tensor_tensor(A[:, c, :], f0[:, :], f1[:, :], ALU.add),
                    reads=[rf["f0"], rf["f1"]], writes=[rA[c]])
        if DBG:
            for k in range(32):
                s = k % 2
                dve(lambda e, s=s, k=k: e.tensor_copy(xst[:, s, :], A[:, k, :]), reads=[rA[k]], writes=[rxst[s]])
                S.dma("sp", cxst[s], dbg["z"][:, k, :], xst[:, s, :], reads=[rxst[s]])

        Wo = wv(w_out)
        H0 = E - 2
        for j in range(16):
            pieces = wpieces(Wo, KC, 256 * j)
            for f in range(2):
                c = 2 * j + f
                s = c % 2
                S.dma("sp", cxst[s], xst[:, s, :], xT_ext[c * 128:(c + 1) * 128, :], writes=[rxst[s]])
                banks = proj_A(pieces, f, A, rA[0:KC], 0, TT3)
                for bi, (t0, tn) in enumerate(TT3):
                    dve(lambda e, bi=bi, t0=t0, tn=tn, banks=banks, s=s: e.tensor_tensor(xst[:, s, t0:t0 + tn], ps[:, banks[bi], 0:tn], xst[:, s, t0:t0 + tn], ALU.add),
                        reads=[rbank[banks[bi]], rxst[s]], writes=[rxst[s]])
                    bank_free(banks[bi])
                all_out_toks.append(S.dma("sp", cout[s], outT[c * 128:(c + 1) * 128, :], xst[:, s, E:NT], reads=[rxst[s]]))
                dve(lambda e, s=s, c=c: e.tensor_scalar(BC[:, c, 0:NF], xst[:, s, H0:NT], P("g2", c), None, ALU.mult),
                    reads=[rxst[s], rf["prm"]], writes=[rBC[c]])
                if c == 0:
                    act(lambda e, s=s: e.activation(f3[:, 0:NF], xst[:, s, H0:NT], AF.Square), reads=[rxst[s]], writes=[rf["f3"]])
                else:
                    act(lambda e, s=s: e.activation(f5[:, 0:NF], xst[:, s, H0:NT], AF.Square), reads=[rxst[s]], writes=[rf["f5"]])
                    dve(lambda e: e.tensor_tensor(f3[:, 0:NF], f3[:, 0:NF], f5[:, 0:NF], ALU.add),
                        reads=[rf["f5"], rf["f3"]], writes=[rf["f3"]])
        colsum_to(f4, rf["f4"], f3, rf["f3"], TF3, 1.0 / D, rstd_chain(f4, rf["f4"], 1.0 / D, EPS))
        dve(lambda e: e.tensor_scalar(f4[:, 0:2], f4[:, 0:2], P("flag"), None, ALU.mult), reads=[rf["f4"], rf["prm"]], writes=[rf["f4"]])

        Wfi = wv(w_fi)
        Wfo = wv(w_fo)
        NPAIR = DFF // 256
        groups = [(0, 11), (11, 11), (22, 11), (33, 10)]
        sg = [f0, f1]
        rsg = [rf["f0"], rf["f1"]]

        def ffn_conv(banks, fch, dst, dres):
            for bi, (t0, tn) in enumerate(TF3):
                dve(lambda e, bi=bi, t0=t0, tn=tn: e.tensor_tensor(f2[:, t0:t0 + tn], ps[:, banks[bi], 0:tn], f4[:, t0:t0 + tn], ALU.mult),
                    reads=[rbank[banks[bi]], rf["f4"]], writes=[rf["f2"]])
                bank_free(banks[bi])
            dve(lambda e: e.tensor_scalar(dst[:, 0:NO], f2[:, 0:NO], P("fcw", fch * 3), P("fcb", fch), ALU.mult, ALU.add),
                reads=[rf["f2"], rf["prm"]], writes=[dres])
            for tap in (1, 2):
                dve(lambda e, tap=tap: e.scalar_tensor_tensor(dst[:, 0:NO], f2[:, tap:tap + NO], P("fcw", fch * 3 + tap), dst[:, 0:NO], ALU.mult, ALU.add),
                    reads=[rf["f2"], dres, rf["prm"]], writes=[dres])

        pending = []
        for (p0, npair) in groups:
            ng = 2 * npair
            for jj in range(npair):
                j = p0 + jj
                pg = wpieces(Wfi, KC, 256 * j)
                for f in range(2):
                    banks = proj_A(pg, f, BC, rBC[0:KC], 0, TF3)
                    ffn_conv(banks, 2 * j + f, sg[f], rsg[f])
                    act(lambda e, f=f: e.activation(sg[f][:, 0:NO], sg[f][:, 0:NO], AF.Silu), reads=[rsg[f]], writes=[rsg[f]])
                pu = wpieces(Wfi, KC, DFF + 256 * j)
                for f in range(2):
                    banks = proj_A(pu, f, BC, rBC[0:KC], 0, TF3)
                    ffn_conv(banks, 86 + 2 * j + f, f3, rf["f3"])
                    ai = 2 * jj + f
                    dve(lambda e, f=f, ai=ai: e.tensor_tensor(A[:, ai, 0:NO], sg[f][:, 0:NO], f3[:, 0:NO], ALU.mult),
                        reads=[rsg[f], rf["f3"]], writes=[rA[ai]])
            for jo in range(16):
                pcs = []
                k0 = 0
                while k0 < ng:
                    kc = min(KPIECE, ng - k0)
                    pcs.append((wload(Wfo, 2 * p0 + k0, kc, 256 * jo), k0, kc))
                    k0 += kc
                for (tokp) in pending:
                    tokp()
                pending = []
                for f in range(2):
                    c = 2 * jo + f
                    s = c % 2
                    banks = proj_A(pcs, f, A, rA[0:ng], 0, T2)
                    for bi, (t0, tn) in enumerate(T2):
                        act(lambda e, bi=bi, t0=t0, tn=tn, banks=banks, s=s: e.activation(xst[:, s, t0:t0 + tn], ps[:, banks[bi], 0:tn], AF.Copy),
                            reads=[rbank[banks[bi]]], writes=[rxst[s]])
                        bank_free(banks[bi])

                    def mk(c=c, s=s):
                        def go():
                            all_out_toks.append(S.dma("pool", cout[s], outT[c * 128:(c + 1) * 128, :], xst[:, s, 0:NO],
                                                      reads=[rxst[s]], accum_op=ALU.add))
                        return go
                    pending.append(mk())
        for tokp in pending:
            tokp()
        S.wait_all("sp", all_out_toks[-4:])
        S.wait_all("pool", all_out_toks[-4:])

        with nc.Block() as block:
            @block.tensor
            def _(e):
                S.emit("pe", e)

            @block.scalar
            def _(e):
                S.emit("act", e)

            @block.vector
            def _(e):
                S.emit("dve", e)

            @block.gpsimd
            def _(e):
                S.emit("pool", e)

            @block.sync
            def _(e):
                S.emit("sp", e)
    return nc


def _lay(v, nchunk):
    return np.ascontiguousarray(np.asarray(v, np.float32).reshape(nchunk, 128).T)


_NC_CACHE = {}


def kernel(x, positions, norm1_g, w_in, q_norm_g, k_norm_g, w_o_attn, conv_w, conv_b, conv_ln_g, conv_ln_b,
           w_o_conv, w_out, norm2_g, w_ffn_in, ffn_conv_w, ffn_conv_b, w_ffn_out):
    x = np.asarray(x, np.float32)
    positions = np.asarray(positions, np.int32)
    B, Sq, _ = x.shape
    l = 0
    ident = np.eye(128, dtype=np.float32)
    rot = np.zeros((128, 128), np.float32)
    for m in range(64):
        rot[m + 64, m] = -1.0
        rot[m, m + 64] = 1.0
    tri = np.where(np.arange(128)[:, None] <= np.arange(128)[None, :], 0.0, -BIG).astype(np.float32)
    cst = np.ascontiguousarray(np.concatenate([ident, rot, tri], axis=1))
    half = HD // 2
    inv_freq = (np.float32(10000.0) ** (-np.arange(half, dtype=np.float32) / np.float32(half))).astype(np.float32)
    invf = np.concatenate([inv_freq, inv_freq]).reshape(128, 1)

    prm_base = np.zeros((128, NPRM), np.float32)

    def put(name, arr):
        arr = np.asarray(arr, np.float32)
        prm_base[:, _cols[name]:_cols[name] + arr.shape[1]] = arr

    put("g1", _lay(norm1_g[l], 32))
    put("g2", _lay(norm2_g[l], 32))
    put("qg", np.asarray(q_norm_g[l], np.float32).reshape(128, 1))
    put("kg", np.asarray(k_norm_g[l], np.float32).reshape(128, 1))
    cw = np.asarray(conv_w[l], np.float32)
    put("convw", np.ascontiguousarray(cw.T.reshape(16, 128, 31).transpose(1, 0, 2)).reshape(128, 16 * 31))
    put("convb", _lay(conv_b[l], 16))
    put("lng", _lay(conv_ln_g[l], 16))
    put("lnb", _lay(conv_ln_b[l], 16))
    fw = np.asarray(ffn_conv_w[l], np.float32)
    put("fcw", np.ascontiguousarray(fw.T.reshape(172, 128, 3).transpose(1, 0, 2)).reshape(128, 172 * 3))
    put("fcb", _lay(ffn_conv_b[l], 172))
    put("invf", invf)
    rowqk = np.concatenate([np.asarray(q_norm_g[l], np.float32), np.asarray(k_norm_g[l], np.float32)]).reshape(1, 256)

    ws = {
        "w_in": np.ascontiguousarray(np.asarray(w_in[l], np.float32)),
        "w_o_attn": np.ascontiguousarray(np.asarray(w_o_attn[l], np.float32)),
        "w_o_conv": np.ascontiguousarray(np.asarray(w_o_conv[l], np.float32)),
        "w_out": np.ascontiguousarray(np.asarray(w_out[l], np.float32)),
        "w_ffn_in": np.ascontiguousarray(np.asarray(w_ffn_in[l], np.float32)),
        "w_ffn_out": np.ascontiguousarray(np.asarray(w_ffn_out[l], np.float32)),
    }
    in_maps = []
    for c in range(8):
        b, h = c // 2, c % 2
        s = h * NO
        xT = x[b].T
        pos = positions[b]
        xT_ext = np.zeros((D, NT), np.float32)
        pos_e = np.zeros((NT,), np.int32)
        xT_prev = np.zeros((D, NP), np.float32)
        pos_p = np.zeros((NP,), np.int32)
        if h == 0:
            xT_ext[:, E:] = xT[:, 0:NO]
            pos_e[E:] = pos[0:NO]
        else:
            xT_ext[:, :] = xT[:, s - E:s + NO]
            pos_e[:] = pos[s - E:s + NO]
            xT_prev[:, :] = xT[:, s - NP:s]
            pos_p[:] = pos[s - NP:s]
        prm = prm_base.copy()
        prm[:, _cols["flag"]] = float(h)
        mask = np.zeros((4, 8), np.float32)
        for i in range(4):
            for jn in range(8):
                if jn < 4:
                    mask[i, jn] = 0.0 if h == 1 else -BIG
                else:
                    mask[i, jn] = 0.0 if (jn - 4) < i else -BIG
        prm[:, _cols["mask"]:_cols["mask"] + 32] = mask.reshape(1, 32)
        m = {
            "xT_ext": xT_ext, "xT_prev": xT_prev,
            "pos_ext": np.ascontiguousarray(np.broadcast_to(pos_e[None, :], (128, NT))),
            "pos_prev": np.ascontiguousarray(np.broadcast_to(pos_p[None, :], (128, NP))),
            "prm": prm, "cst": cst, "rowqk": rowqk,
        }
        m.update(ws)
        in_maps.append(m)
    if "nc" not in _NC_CACHE:
        _NC_CACHE["nc"] = build_nc()
    nc = _NC_CACHE["nc"]
    res = run_bass_kernel_spmd(nc, in_maps, core_ids=list(range(8)))
    out = np.empty((B, Sq, D), np.float32)
    for c in range(8):
        b, h = c // 2, c % 2
        out[b, h * NO:(h + 1) * NO, :] = res.results[c]["outT"].T
    if DBG:
        kernel.dbg = res.results
    return out
```
